# Optimizing a Trainium2 kernel written in Bass

```python
import jax, jax.numpy as jnp
from jax import lax
import numpy as np

D_MODEL = 1024
BATCH = 8
SEQ = 2048
DEPTH = 2

HEAD_DIM = 64
DIL_PATTERNS = ((128, 1), (512, 4), (2048, 16))
DIL_HEADS = 4
DIL_WIDTH = len(DIL_PATTERNS) * DIL_HEADS * HEAD_DIM
DIL_OUT = DIL_HEADS * HEAD_DIM
SWA_WINDOW = 128
SWA_Q_HEADS = 8
SWA_KV_HEADS = 2
SWA_GROUP = SWA_Q_HEADS // SWA_KV_HEADS
HGRN_HEADS = 4
HGRN_DK = 128
HGRN_DV = 128
HGRN_CHUNK = 32
N_BRANCH = 3
D_FF = 4 * D_MODEL
CONV_WIDTH = 3
BAND_BLOCK = 128
EPS = 1e-6
SPLITS = (DIL_WIDTH, DIL_WIDTH, DIL_WIDTH,
          SWA_Q_HEADS * HEAD_DIM, SWA_KV_HEADS * HEAD_DIM, SWA_KV_HEADS * HEAD_DIM,
          HGRN_HEADS * HGRN_DK, HGRN_HEADS * HGRN_DK, HGRN_HEADS * HGRN_DV, HGRN_HEADS * HGRN_DV,
          N_BRANCH * D_MODEL)
D_IN = sum(SPLITS)

kernel_name = "hybrid_dilated_swa_hgrn2_block"


def rms_norm(x, g):
    xf = x.astype(jnp.float32)
    y = xf * lax.rsqrt(jnp.mean(xf * xf, axis=-1, keepdims=True) + EPS)
    return (y * g.astype(jnp.float32)).astype(x.dtype)


def banded_attention(q, k, v, max_dist, sink=None):
    Bsz, L, Hkv, G, Dh = q.shape
    nb = -(-L // BAND_BLOCK)
    pad = nb * BAND_BLOCK - L
    if pad:
        q = jnp.pad(q, ((0, 0), (0, pad), (0, 0), (0, 0), (0, 0)))
        k = jnp.pad(k, ((0, 0), (0, pad), (0, 0), (0, 0)))
        v = jnp.pad(v, ((0, 0), (0, pad), (0, 0), (0, 0)))
    qb = q.reshape(Bsz, nb, BAND_BLOCK, Hkv, G, Dh)
    kb = k.reshape(Bsz, nb, BAND_BLOCK, Hkv, Dh)
    vb = v.reshape(Bsz, nb, BAND_BLOCK, Hkv, Dh)
    shift = lambda t: jnp.pad(t, ((0, 0), (1, 0), (0, 0), (0, 0), (0, 0)))[:, :-1]
    kcat = jnp.concatenate([shift(kb), kb], axis=2)
    vcat = jnp.concatenate([shift(vb), vb], axis=2)
    s = jnp.einsum('bnqhgd,bnkhd->bnhgqk', qb, kcat).astype(jnp.float32) * (Dh ** -0.5)
    qi = jnp.arange(BAND_BLOCK)[:, None]
    ki = jnp.arange(2 * BAND_BLOCK)[None, :]
    dist = BAND_BLOCK + qi - ki
    band = (dist >= 0) & (dist <= max_dist)
    not_before_start = (jnp.arange(nb)[:, None, None] > 0) | (ki >= BAND_BLOCK)[None]
    valid = band[None] & not_before_start
    s = jnp.where(valid[None, :, None, None], s, -jnp.inf)
    m = jnp.max(s, axis=-1)
    if sink is not None:
        sk = sink.astype(jnp.float32)[None, None, :, :, None]
        m = jnp.maximum(m, sk)
    p = jnp.exp(s - m[..., None])
    den = jnp.sum(p, axis=-1)
    if sink is not None:
        den = den + jnp.exp(sk - m)
    o = jnp.einsum('bnhgqk,bnkhd->bnqhgd', p, vcat.astype(jnp.float32))
    o = o / jnp.moveaxis(den, -1, 2)[..., None]
    lse = jnp.moveaxis(m + jnp.log(den), -1, 2)
    o = o.reshape(Bsz, nb * BAND_BLOCK, Hkv, G, Dh)[:, :L].astype(q.dtype)
    lse = lse.reshape(Bsz, nb * BAND_BLOCK, Hkv, G)[:, :L]
    return o, lse


def dilated_group(q, k, v, window, dilation):
    Bsz, S, H, Dh = q.shape
    L = S // dilation
    fold = lambda t: t.reshape(Bsz, L, dilation * H, Dh)
    o, lse = banded_attention(fold(q)[:, :, :, None], fold(k), fold(v), window // dilation)
    return o.reshape(Bsz, S, H, Dh), lse.reshape(Bsz, S, H)


def hgrn2(q, fz, i, lb):
    Bsz, S, H, Dk = q.shape
    Dv = i.shape[-1]
    C = HGRN_CHUNK
    N = S // C
    q = jax.nn.silu(q.astype(jnp.float32))
    fz = fz.astype(jnp.float32)
    lb = lb.astype(jnp.float32)
    logf = jnp.logaddexp(jnp.log(lb), jnp.log1p(-lb) + jax.nn.log_sigmoid(fz))
    k = (1.0 - lb) * jax.nn.sigmoid(-fz)
    rs = lambda t: t.reshape(Bsz, N, C, H, t.shape[-1])
    q, k, logf, v = rs(q), rs(k), rs(logf), rs(i.astype(jnp.float32))
    b = jnp.cumsum(logf, axis=2)
    b_last = b[:, :, -1]
    qe = q * jnp.exp(b)
    ke = k * jnp.exp(-b)
    kd = k * jnp.exp(b_last[:, :, None] - b)
    causal = jnp.tril(jnp.ones((C, C), dtype=bool))
    att = jnp.einsum('bnthk,bnshk->bnhts', qe, ke)
    att = jnp.where(causal, att, 0.0)
    o_intra = jnp.einsum('bnhts,bnshv->bnthv', att, v)
    U = jnp.einsum('bnshk,bnshv->bnhkv', kd, v)
    decay = jnp.exp(b_last)

    def step(state, inp):
        d, u = inp
        return d[..., None] * state + u, state

    init = jnp.zeros((Bsz, H, Dk, Dv), jnp.float32)
    _, s_prev = lax.scan(step, init, (jnp.moveaxis(decay, 1, 0), jnp.moveaxis(U, 1, 0)))
    s_prev = jnp.moveaxis(s_prev, 0, 1)
    o_inter = jnp.einsum('bnthk,bnhkv->bnthv', qe, s_prev)
    return (o_intra + o_inter).reshape(Bsz, S, H, Dv)


def token_mixer(hn, w_in, sinks, lb, out_gain, w_a, w_b, w_c, w_o):
    Bsz, S, _ = hn.shape
    idx = np.cumsum(np.array(SPLITS))[:-1].tolist()
    proj = hn @ w_in
    aq, ak, av, bq, bk, bv, cq, cf, ci, cg, gates = jnp.split(proj, idx, axis=-1)
    grp = lambda t: t.reshape(Bsz, S, len(DIL_PATTERNS), DIL_HEADS, HEAD_DIM)
    aq, ak, av = grp(aq), grp(ak), grp(av)
    outs, lses = [], []
    for gi, (win, dil) in enumerate(DIL_PATTERNS):
        o, lse = dilated_group(aq[:, :, gi], ak[:, :, gi], av[:, :, gi], win, dil)
        outs.append(o)
        lses.append(lse)
    alpha = jax.nn.softmax(jnp.stack(lses, 0), axis=0)
    oa = jnp.sum(alpha[..., None] * jnp.stack(outs, 0).astype(jnp.float32), axis=0)
    ya = oa.astype(hn.dtype).reshape(Bsz, S, DIL_OUT) @ w_a
    ob, _ = banded_attention(bq.reshape(Bsz, S, SWA_KV_HEADS, SWA_GROUP, HEAD_DIM),
                             bk.reshape(Bsz, S, SWA_KV_HEADS, HEAD_DIM),
                             bv.reshape(Bsz, S, SWA_KV_HEADS, HEAD_DIM),
                             SWA_WINDOW - 1, sink=sinks.reshape(SWA_KV_HEADS, SWA_GROUP))
    yb = ob.reshape(Bsz, S, SWA_Q_HEADS * HEAD_DIM) @ w_b
    hd = lambda t, d: t.reshape(Bsz, S, HGRN_HEADS, d)
    oc = hgrn2(hd(cq, HGRN_DK), hd(cf, HGRN_DK), hd(ci, HGRN_DV), lb.reshape(HGRN_HEADS, HGRN_DK))
    oc = rms_norm(oc.astype(hn.dtype), out_gain) * jax.nn.silu(hd(cg, HGRN_DV))
    yc = oc.reshape(Bsz, S, HGRN_HEADS * HGRN_DV) @ w_c
    g = jax.nn.sigmoid(gates.reshape(Bsz, S, N_BRANCH, D_MODEL))
    mix = g[:, :, 0] * ya + g[:, :, 1] * yb + g[:, :, 2] * yc
    return mix @ w_o


def causal_dwconv(u, w, b):
    K, C = w.shape
    y = lax.conv_general_dilated(u, w.astype(u.dtype)[:, None, :], window_strides=(1,),
                                 padding=[(K - 1, 0)], dimension_numbers=('NWC', 'WIO', 'NWC'),
                                 feature_group_count=C)
    return y + b.astype(u.dtype)


def conv_ffn(hn, w_up, conv_w, conv_b, w_down):
    ab = causal_dwconv(hn @ w_up, conv_w, conv_b)
    a, b = jnp.split(ab, 2, axis=-1)
    return (jax.nn.gelu(a, approximate=True) * b) @ w_down


def setup_inputs(seed: int = 0) -> dict:
    key = jax.random.key(seed)
    ks = jax.random.split(key, 18)
    f32 = jnp.float32
    nrm = lambda k, shape, scale: jax.random.normal(k, shape, f32) * scale
    gain = lambda k, shape: 1.0 + 0.02 * jax.random.normal(k, shape, f32)
    return {
        "x": jax.random.normal(ks[0], (BATCH, SEQ, D_MODEL), f32),
        "norm_pre_mix": gain(ks[1], (DEPTH, D_MODEL)),
        "norm_post_mix": gain(ks[2], (DEPTH, D_MODEL)),
        "norm_pre_ffn": gain(ks[3], (DEPTH, D_MODEL)),
        "norm_post_ffn": gain(ks[4], (DEPTH, D_MODEL)),
        "w_in": nrm(ks[5], (DEPTH, D_MODEL, D_IN), D_MODEL ** -0.5),
        "attn_sinks": nrm(ks[6], (DEPTH, SWA_Q_HEADS), 0.5),
        "hgrn_lb_logits": nrm(ks[7], (DEPTH, HGRN_HEADS * HGRN_DK), 1.0),
        "hgrn_out_norm": gain(ks[8], (DEPTH, HGRN_DV)),
        "w_branch_a": nrm(ks[9], (DEPTH, DIL_OUT, D_MODEL), DIL_OUT ** -0.5),
        "w_branch_b": nrm(ks[10], (DEPTH, SWA_Q_HEADS * HEAD_DIM, D_MODEL), (SWA_Q_HEADS * HEAD_DIM) ** -0.5),
        "w_branch_c": nrm(ks[11], (DEPTH, HGRN_HEADS * HGRN_DV, D_MODEL), (HGRN_HEADS * HGRN_DV) ** -0.5),
        "w_out": nrm(ks[12], (DEPTH, D_MODEL, D_MODEL), D_MODEL ** -0.5),
        "w_ffn_up": nrm(ks[13], (DEPTH, D_MODEL, 2 * D_FF), D_MODEL ** -0.5),
        "ffn_conv_w": nrm(ks[14], (DEPTH, CONV_WIDTH, 2 * D_FF), CONV_WIDTH ** -0.5),
        "ffn_conv_b": nrm(ks[15], (DEPTH, 2 * D_FF), 0.01),
        "w_ffn_down": nrm(ks[16], (DEPTH, D_FF, D_MODEL), D_FF ** -0.5),
    }


def reference(x, norm_pre_mix, norm_post_mix, norm_pre_ffn, norm_post_ffn, w_in, attn_sinks,
              hgrn_lb_logits, hgrn_out_norm, w_branch_a, w_branch_b, w_branch_c, w_out,
              w_ffn_up, ffn_conv_w, ffn_conv_b, w_ffn_down):
    lbs = jnp.cumsum(jax.nn.softmax(hgrn_lb_logits.astype(jnp.float32), axis=0), axis=0)
    lbs = lbs - lbs[0:1]
    h = x
    for l in range(DEPTH):
        mix = token_mixer(rms_norm(h, norm_pre_mix[l]), w_in[l], attn_sinks[l], lbs[l],
                          hgrn_out_norm[l], w_branch_a[l], w_branch_b[l], w_branch_c[l], w_out[l])
        h = h + rms_norm(mix, norm_post_mix[l])
        ff = conv_ffn(rms_norm(h, norm_pre_ffn[l]), w_ffn_up[l], ffn_conv_w[l], ffn_conv_b[l], w_ffn_down[l])
        h = h + rms_norm(ff, norm_post_ffn[l])
    return h
```

```python
import contextlib
import numpy as np
import concourse.bass as bass
import concourse.mybir as mybir
from concourse.bass_utils import run_bass_kernel_spmd
from concourse.alu_op_type import AluOpType as ALU

F32 = mybir.dt.float32
BF16 = mybir.dt.bfloat16
AF = mybir.ActivationFunctionType

S = 2048
D = 1024
NT = 16
NB = 4
TB = 512
KC = 8
DFF = 4096
EPS = 1e-6
NEG = -30000.0

ENGS = ["pe", "act", "dve", "pool", "sp"]
EPOCH = 30000


class T:
    __slots__ = ("name", "w", "r")

    def __init__(self, name):
        self.name = name
        self.w = None
        self.r = {}


class Prog:
    def __init__(self):
        self.streams = {e: [] for e in ENGS}
        self.count = {e: 0 for e in ENGS}
        self.seen = {e: {} for e in ENGS}
        self.dma_count = {}
        self.semkeys = set()
        self.n_dma_sems = 0

    def new_dma_sem(self):
        self.n_dma_sems += 1
        k = ("dma", self.n_dma_sems)
        self.dma_count[k] = 0
        self.semkeys.add(k)
        return k

    def _collect(self, eng, ident, reads, writes):
        waits = {}

        def need(dep, skip_same):
            if dep is None:
                return
            semkey, val, did = dep
            if skip_same and did == ident:
                return
            if self.seen[eng].get(semkey, 0) >= val:
                return
            if waits.get(semkey, 0) < val:
                waits[semkey] = val

        for t in reads:
            need(t.w, ident == "pe")
        for t in writes:
            need(t.w, True)
            for rid, (sk, v) in t.r.items():
                need((sk, v, rid), True)
        for sk, v in waits.items():
            self.seen[eng][sk] = v
        return list(waits.items())

    def op(self, eng, fn, reads=(), writes=()):
        waits = self._collect(eng, eng, reads, writes)
        c = self.count[eng]
        semkey = (eng, c // EPOCH)
        val = c % EPOCH + 1
        self.count[eng] = c + 1
        self.semkeys.add(semkey)
        self.streams[eng].append((waits, fn, semkey, 1))
        for t in reads:
            t.r[eng] = (semkey, val)
        for t in writes:
            t.w = (semkey, val, eng)
            t.r = {}

    def dma(self, q, fn, semkey, reads=(), writes=()):
        ident = semkey
        waits = self._collect(q, ident, reads, writes)
        self.dma_count[semkey] += 16
        val = self.dma_count[semkey]
        self.streams[q].append((waits, fn, semkey, 16))
        for t in reads:
            t.r[ident] = (semkey, val)
        for t in writes:
            t.w = (semkey, val, ident)
            t.r = {}

    @staticmethod
    def inherit(new_t, olds):
        for o in olds:
            if o.w is not None:
                new_t.r[("w", o.w[2], o.w[0])] = (o.w[0], o.w[1])
            for rid, (sk, v) in o.r.items():
                new_t.r[("r", rid, sk)] = (sk, v)

    def barrier(self, engs, dma=True):
        cur = []
        for f in ENGS:
            c = self.count[f]
            if c > 0:
                cur.append(((f, (c - 1) // EPOCH), (c - 1) % EPOCH + 1, f))
        if dma:
            for k, v in self.dma_count.items():
                if v > 0:
                    cur.append((k, v, k))
        for e in engs:
            waits = []
            for sk, v, f in cur:
                if f == e:
                    continue
                if self.seen[e].get(sk, 0) >= v:
                    continue
                self.seen[e][sk] = v
                waits.append((sk, v))
            if waits:
                self.streams[e].append((waits, None, None, 0))

    def emit(self, nc, stack):
        sems = {}
        for k in sorted(self.semkeys, key=str):
            sems[k] = stack.enter_context(nc.semaphore("s_%s_%s" % (k[0], k[1])))
        block = stack.enter_context(nc.Block())
        names = {"pe": "tensor", "act": "scalar", "dve": "vector", "pool": "gpsimd", "sp": "sync"}

        def run(engname):
            def body(e):
                for waits, fn, semkey, inc in self.streams[engname]:
                    for sk, v in waits:
                        e.wait_ge(sems[sk], v)
                    if fn is not None:
                        ins = fn(e)
                        ins.then_inc(sems[semkey], inc)
            return body

        for en in ENGS:
            if self.streams[en]:
                getattr(block, names[en])(run(en))


def f_mms(items):
    def fn(e):
        ins = None
        for (o, l, r, s, t, kw) in items:
            ins = e.matmul(o, lhsT=l, rhs=r, start=s, stop=t, **kw)
        return ins
    return fn


def f_act(out, in_, func, **kw):
    return lambda e: e.activation(out=out, in_=in_, func=func, **kw)


def f_acopy(out, in_):
    return lambda e: e.copy(out=out, in_=in_)


def f_tt(out, in0, in1, op):
    return lambda e: e.tensor_tensor(out=out, in0=in0, in1=in1, op=op)


def f_ts(out, in0, s1, s2, op0, op1):
    return lambda e: e.tensor_scalar(out=out, in0=in0, scalar1=s1, scalar2=s2, op0=op0, op1=op1)


def f_stt(out, in0, scalar, in1, op0, op1):
    return lambda e: e.scalar_tensor_tensor(out=out, in0=in0, scalar=scalar, in1=in1, op0=op0, op1=op1)


def f_copy(out, in_):
    return lambda e: e.tensor_copy(out=out, in_=in_)


def f_memset(ap, v):
    return lambda e: e.memset(ap, v)


def f_trs(items):
    def fn(e):
        ins = None
        for (o, i, idn) in items:
            ins = e.transpose(o, i, idn)
        return ins
    return fn


def f_dma(out, in_):
    return lambda e: e.dma_start(out=out, in_=in_)


def f_dmas(pairs):
    def fn(e):
        ins = None
        for (o, i) in pairs:
            ins = e.dma_start(out=o, in_=i)
        return ins
    return fn


C_ID = 0
C_MASK = 128
C_BLK = C_MASK + 8 * 128
C_VBLK = C_BLK + 128
C_RST = C_VBLK + 512
C_ONES = C_RST + 512
NCONST = C_ONES + 128

M_CAUSAL, M_OFFB, M_OFFA, M_4C, M_4, M_4U, M_16C, M_16 = range(8)

V_PER_L = 32 + 256 + 1 + 8
V_NPM, V_NPO, V_NPF, V_NPFF, V_CW, V_CB, V_OG, V_SK = 0, 8, 16, 24, 32, 224, 288, 289
V_LB = 2 * V_PER_L
NVEC = V_LB + 8


def make_consts():
    c = np.zeros((128, NCONST), np.float32)
    k = np.arange(128)[:, None]
    q = np.arange(128)[None, :]
    c[:, C_ID:C_ID + 128] = (k == q)
    valid = [None] * 8
    valid[M_CAUSAL] = (q >= k)
    valid[M_OFFB] = (q < k)
    valid[M_OFFA] = (q <= k)
    m4 = ((q - k) % 4 == 0)
    m16 = ((q - k) % 16 == 0)
    valid[M_4C] = m4 & (q >= k)
    valid[M_4] = m4
    valid[M_4U] = m4 & (q <= k)
    valid[M_16C] = m16 & (q >= k)
    valid[M_16] = m16
    for i in range(8):
        c[:, C_MASK + i * 128:C_MASK + (i + 1) * 128] = np.where(valid[i], 1.0, 0.0)
    c[:, C_BLK:C_BLK + 128] = ((k // 32) == (q // 32)) & (k <= q)
    vb = np.zeros((128, 4, 128), np.float32)
    for n in range(4):
        vb[32 * n:32 * n + 32, n, :] = 1.0
    c[:, C_VBLK:C_VBLK + 512] = vb.reshape(128, 512)
    rst = np.ones((128, 512), np.float32)
    rst[:, 0::32] = 0.0
    c[:, C_RST:C_RST + 512] = rst
    c[:, C_ONES:C_ONES + 128] = 1.0
    return c


def make_vecs(inp):
    v = np.zeros((128, NVEC), np.float32)
    pc = lambda a, n: np.ascontiguousarray(a.reshape(n, 128).T)
    for l in range(2):
        b = l * V_PER_L
        v[:, b + V_NPM:b + V_NPM + 8] = pc(inp["norm_pre_mix"][l], 8)
        v[:, b + V_NPO:b + V_NPO + 8] = pc(inp["norm_post_mix"][l], 8)
        v[:, b + V_NPF:b + V_NPF + 8] = pc(inp["norm_pre_ffn"][l], 8)
        v[:, b + V_NPFF:b + V_NPFF + 8] = pc(inp["norm_post_ffn"][l], 8)
        cw = inp["ffn_conv_w"][l]
        for j in range(3):
            v[:, b + V_CW + j * 64:b + V_CW + (j + 1) * 64] = pc(cw[j], 64)
        v[:, b + V_CB:b + V_CB + 64] = pc(inp["ffn_conv_b"][l], 64)
        v[:, b + V_OG] = inp["hgrn_out_norm"][l]
        v[:, b + V_SK:b + V_SK + 8] = inp["attn_sinks"][l][None, :]
        v[:, V_LB + l * 4:V_LB + l * 4 + 4] = pc(inp["hgrn_lb_logits"][l], 4)
    return v


SLOT_EL = 4352
ARENA_BYTES = 116 * 1024


def build(n_layers=2, dbg=()):
    nc = bass.Bass("TRN2", target_bir_lowering=False)
    dram = lambda n, s, dt=F32, kind="ExternalInput": nc.dram_tensor(n, s, dt, kind=kind).ap()
    xT = dram("xT", [D, S])
    w_in = dram("w_in", [2, D, 8192])
    w_a = dram("w_a", [2, 256, D])
    w_b = dram("w_b", [2, 512, D])
    w_c = dram("w_c", [2, 512, D])
    w_o = dram("w_o", [2, D, D])
    w_up = dram("w_up", [2, D, 8192])
    w_dn = dram("w_dn", [2, DFF, D])
    vecs_d = dram("vecs", [128, NVEC])
    cst_d = dram("consts", [128, NCONST])
    outT = dram("outT", [D, S], kind="ExternalOutput")
    dbg_out = {}

    st = contextlib.ExitStack()
    with st:
        sb = lambda n, s, d: st.enter_context(nc.sbuf_tensor(n, s, d))
        hT = sb("hT", [128, KC, S], F32)
        cst = sb("cst", [128, NCONST], BF16)
        vec = sb("vec", [128, NVEC], F32)
        small = sb("small", [128, 64], F32)
        slots = [sb("slot%d" % i, [128, SLOT_EL], BF16) for i in range(2)]
        arena = sb("arena", [128, ARENA_BYTES // 2], BF16)
        banks = [st.enter_context(nc.psum_tensor("pb%d" % i, [128, 512], F32)) for i in range(8)]

        P = Prog()
        t_bank = [T("pb%d" % i) for i in range(8)]
        t_h = [T("h%d" % i) for i in range(NB)]
        t_cst = T("cst")
        t_vec = T("vec")
        t_small = T("small")

        ident = cst[:, C_ID:C_ID + 128]
        ones_bf = cst[:, C_ONES:C_ONES + 128]
        mask_ap = lambda i: cst[:, C_MASK + i * 128:C_MASK + (i + 1) * 128]
        blk01 = cst[:, C_BLK:C_BLK + 128]
        vblk = cst[:, C_VBLK:C_VBLK + 512]
        rstm = cst[:, C_RST:C_RST + 512]
        eps_ap = small[:, 0:1]

        class Arena:
            def __init__(self):
                self.off = 0

            def alloc(self, nbytes):
                o = self.off
                self.off = (self.off + nbytes + 63) // 64 * 64
                assert self.off <= ARENA_BYTES, ("arena overflow", self.off)
                return o

            def f32(self, off, n):
                return arena[:, off // 2:off // 2 + 2 * n].bitcast(F32)

            def bf(self, off, n):
                return arena[:, off // 2:off // 2 + n]

        AR = Arena()

        class Ring:
            def __init__(self, aps):
                self.aps = list(aps)
                self.ts = [T("slot%d" % i) for i in range(len(aps))]
                self.sems = [P.new_dma_sem() for _ in aps]
                self.i = 0

            def load(self, pairs_fn):
                k = self.i % len(self.aps)
                self.i += 1
                ap = self.aps[k]
                for (o_, i_) in pairs_fn(ap):
                    if len(o_.shape) > 3:
                        for a_ in range(o_.shape[2]):
                            P.dma("pool", f_dma(o_[:, :, a_, :], i_[:, :, a_, :]), self.sems[k], writes=[self.ts[k]])
                    else:
                        P.dma("pool", f_dma(o_, i_), self.sems[k], writes=[self.ts[k]])
                return ap, self.ts[k]

        ring = Ring([s_[:] for s_ in slots])

        def dbg_dump(name, ap, t, shape):
            if name in dbg:
                d = nc.dram_tensor("dbg_" + name, shape, F32, kind="ExternalOutput").ap()
                dbg_out[name] = d
                P.dma("pool", f_dma(d, ap), P.new_dma_sem(), reads=[t], writes=[T("dbgo")])

        s_in = P.new_dma_sem()
        P.dma("sp", f_dma(vec[:], vecs_d), P.new_dma_sem(), writes=[t_vec])
        P.dma("pool", f_dma(cst[:], cst_d), P.new_dma_sem(), writes=[t_cst])
        for tb in range(NB):
            s_tb = P.new_dma_sem()
            for c in range(KC):
                P.dma("sp", f_dma(hT[:, c, tb * TB:(tb + 1) * TB], xT[c * 128:(c + 1) * 128, tb * TB:(tb + 1) * TB]), s_tb,
                      reads=([t_h[0]] if tb > 0 else []), writes=[t_h[tb]])
        P.op("dve", f_memset(small[:], 0.0), writes=[t_small])
        P.op("dve", f_memset(eps_ap, EPS), writes=[t_small])
        lg = vec[:, V_LB:V_LB + 8]
        tmpA = small[:, 48:56]
        tmpB = small[:, 56:60]
        P.op("act", f_act(tmpA, lg, AF.Exp), reads=[t_vec, t_small], writes=[t_small])
        P.op("dve", f_tt(tmpB, tmpA[:, 0:4], tmpA[:, 4:8], ALU.add), reads=[t_small], writes=[t_small])
        P.op("dve", lambda e: e.reciprocal(out=tmpB, in_=tmpB), reads=[t_small], writes=[t_small])
        P.op("dve", f_tt(tmpA[:, 0:4], tmpA[:, 0:4], tmpB, ALU.mult), reads=[t_small], writes=[t_small])
        P.op("dve", f_tt(tmpA[:, 4:8], tmpA[:, 4:8], tmpB, ALU.mult), reads=[t_small], writes=[t_small])
        P.op("dve", f_tt(tmpA[:, 4:8], tmpA[:, 4:8], tmpA[:, 0:4], ALU.add), reads=[t_small], writes=[t_small])
        P.op("dve", f_tt(small[:, 8:12], tmpA[:, 0:4], tmpA[:, 0:4], ALU.subtract), reads=[t_small], writes=[t_small])
        P.op("dve", f_tt(small[:, 12:16], tmpA[:, 4:8], tmpA[:, 0:4], ALU.subtract), reads=[t_small], writes=[t_small])
        P.op("dve", f_ts(small[:, 16:24], small[:, 8:16], -0.5, 0.5, ALU.mult, ALU.add), reads=[t_small], writes=[t_small])
        P.op("dve", f_ts(small[:, 24:32], small[:, 8:16], 0.5, -0.5, ALU.mult, ALU.add), reads=[t_small], writes=[t_small])
        P.op("dve", f_ts(small[:, 8:16], small[:, 8:16], 0.5, 0.5, ALU.mult, ALU.add), reads=[t_small], writes=[t_small])
        for l in range(2):
            sk = vec[:, l * V_PER_L + V_SK:l * V_PER_L + V_SK + 8]
            P.op("act", f_act(small[:, 32 + 8 * l:40 + 8 * l], sk, AF.Exp), reads=[t_vec, t_small], writes=[t_small])

        def rstd_block(src3, nch, ncols, reads, inv_n, sq_ap, t_sq, ln_ap, rstd_ap, t_r, bank):
            P.op("act", f_act(sq_ap, src3, AF.Square), reads=reads, writes=[t_sq])
            items = [(banks[bank][:, 0:ncols], ones_bf, sq_ap[:, c, :], c == 0, c == nch - 1, {}) for c in range(nch)]
            P.op("pe", f_mms(items), reads=[t_sq, t_cst], writes=[t_bank[bank]])
            P.op("act", f_act(ln_ap, banks[bank][:, 0:ncols], AF.Ln, bias=eps_ap, scale=inv_n),
                 reads=[t_bank[bank], t_small], writes=[t_r])
            P.op("act", f_act(rstd_ap, ln_ap, AF.Exp, scale=-0.5), reads=[t_r], writes=[t_r])

        def projT(bank, lhs_fn, rhsT, t_rhs, t_w, tb, ncols=TB, col0=None):
            c0 = tb * TB if col0 is None else col0
            items = [(banks[bank][:, 0:ncols], lhs_fn(kc), rhsT[:, kc, c0:c0 + ncols], kc == 0, kc == KC - 1, {})
                     for kc in range(KC)]
            P.op("pe", f_mms(items), reads=[t_w, t_rhs], writes=[t_bank[bank]])

        for l in range(n_layers):
            vb = l * V_PER_L
            if l > 0:
                P.barrier(["pe", "act", "dve"] + (["pool", "sp"] if dbg else []), dma=bool(dbg))
            o_hn, o_oc, o_oa, o_ob = 0, 32 * 1024, 48 * 1024, 56 * 1024
            AR.off = 72 * 1024
            hnT = AR.bf(o_hn, KC * S).rearrange("p (c t) -> p c t", t=S)
            t_hn = [T("hn%d" % i) for i in range(NB)]
            ocT = AR.bf(o_oc, 4 * S).rearrange("p (c t) -> p c t", t=S)
            oaT = AR.bf(o_oa, 2 * S).rearrange("p (c t) -> p c t", t=S)
            obT = AR.bf(o_ob, 4 * S).rearrange("p (c t) -> p c t", t=S)
            t_oc = [T("oc%d" % i) for i in range(NB)]
            t_oa = [T("oa%d" % i) for i in range(NB)]
            t_ob = [T("ob%d" % i) for i in range(NB)]
            base_mark = AR.off

            AR.off = 48 * 1024
            o_v = AR.alloc(NT * 512 * 2)
            v_tok = AR.bf(o_v, NT * 512).rearrange("p (t c) -> p t c", c=512)
            t_v = T("v_tok")
            NS = 8
            o_Sr = AR.alloc(NS * 128 * 4)
            Sring = AR.f32(o_Sr, NS * 128)
            Sst = [Sring[:, i * 128:(i + 1) * 128] for i in range(NS)]
            t_S = [T("S%d" % i) for i in range(NS)]
            o_Sb = [AR.alloc(512 * 2) for _ in range(2)]
            S_bf = [AR.bf(o, 512) for o in o_Sb]
            t_Sb = [T("Sbf%d" % i) for i in range(2)]
            o_sq1 = AR.alloc(TB * 2)
            sq1 = AR.bf(o_sq1, TB).rearrange("p (c t) -> p c t", c=1)
            t_sq1 = T("sq1")
            o_rs1 = AR.alloc(TB * 4)
            rs1 = AR.f32(o_rs1, TB)
            ln1 = rs1
            t_r1 = T("r1")
            NV = 2
            o_att = [AR.alloc(128 * 2) for _ in range(NV)]
            attm = [AR.bf(o, 128) for o in o_att]
            t_att = [T("attm%d" % i) for i in range(NV)]
            o_sq = o_oc
            sq = AR.bf(o_sq, KC * TB).rearrange("p (c t) -> p c t", t=TB)
            o_rs = o_oc + KC * TB * 2
            rs_pair = [AR.f32(o_rs, TB), AR.f32(o_rs + TB * 4, TB)]
            t_sq = T("sq")
            t_rp = [T("rstd0"), T("rstd1")]
            NX = 2
            Xo = [[AR.alloc(TB * 4) for _ in range(6)] for _ in range(NX)]
            X = [[AR.f32(o, TB) for o in xs] for xs in Xo]
            t_X = [[T("X%d_%d" % (s_, i)) for i in range(6)] for s_ in range(NX)]
            o_X4 = [AR.alloc(TB * 4) for _ in range(1)]
            X4s = [X[0][3], X[1][3], AR.f32(o_X4[0], TB)]
            t_X4s = [t_X[0][3], t_X[1][3], T("X4_2")]
            o_qe = [AR.alloc(TB * 2) for _ in range(NX)]
            QE = [AR.bf(o, TB) for o in o_qe]
            t_QE = [T("QE%d" % i) for i in range(NX)]
            o_ke = [AR.alloc(TB * 2) for _ in range(NX)]
            KE = [AR.bf(o, TB) for o in o_ke]
            t_KE = [T("KE%d" % i) for i in range(NX)]
            o_X7 = AR.alloc(TB * 4)
            X7 = AR.f32(o_X7, TB)
            t_X7 = T("X7")
            o_kd = [AR.alloc(TB * 2) for _ in range(2)]
            KD = [AR.bf(o, TB) for o in o_kd]
            t_KD = [T("KD%d" % i) for i in range(2)]
            o_kt = [AR.alloc(TB * 2) for _ in range(2)]
            kd_tok = [AR.bf(o, TB).rearrange("p (i c) -> p i c", c=128) for o in o_kt]
            t_kt = [T("kdtok%d" % i) for i in range(2)]
            o_vbd = [AR.alloc(512 * 2) for _ in range(NV)]
            vbd = [AR.bf(o, 512) for o in o_vbd]
            t_vbd = [T("vbd%d" % i) for i in range(NV)]

            slab, t_sl = ring.load(lambda ap: [(ap[:, 0:KC * 512].rearrange("p (k c) -> p k c", c=512),
                                               w_in[l].rearrange("(k p) c -> p k c", p=128)[:, :, 4096:4608])])
            slab3 = slab[:, 0:KC * 512].rearrange("p (k c) -> p k c", c=512)
            for tb in range(NB):
                tsl = slice(tb * TB, (tb + 1) * TB)
                rs_t, t_r = rs_pair[tb % 2], t_rp[tb % 2]
                rstd_block(hT[:, :, tsl], KC, TB, [t_h[tb]], 1.0 / D, sq, t_sq, rs_t, rs_t, t_r, 7)
                for c in range(KC):
                    P.op("dve", f_stt(hnT[:, c, tsl], hT[:, c, tsl], vec[:, vb + V_NPM + c:vb + V_NPM + c + 1],
                                      rs_t, ALU.mult, ALU.mult), reads=[t_h[tb], t_r, t_vec], writes=[t_hn[tb]])
                for tile in range(4 * tb, 4 * tb + 4):
                    bk_ = tile % 2
                    items = [(banks[bk_][:, :], hnT[:, kc, tile * 128:(tile + 1) * 128], slab3[:, kc, :], kc == 0, kc == KC - 1, {})
                             for kc in range(KC)]
                    P.op("pe", f_mms(items), reads=[t_sl, t_hn[tile // 4]], writes=[t_bank[bk_]])
                    P.op("act", f_acopy(v_tok[:, tile, :], banks[bk_][:, :]), reads=[t_bank[bk_]], writes=[t_v])
            if l == 0:
                dbg_dump("hn0", hnT, t_hn[3], [128, KC, S])
            for t_ in t_oc:
                P.inherit(t_, [t_sq] + t_rp)

            lb_c = lambda hd: small[:, 8 + 4 * l + hd:9 + 4 * l + hd]
            oml_c = lambda hd: small[:, 16 + 4 * l + hd:17 + 4 * l + hd]
            noml_c = lambda hd: small[:, 24 + 4 * l + hd:25 + 4 * l + hd]
            units = [(hd, blk) for hd in range(4) for blk in range(NB)]
            head_slab = {}
            cstate = {"sidx": 0}

            def S1_parts(u):
                hd, blk = units[u]
                xs = u % NX
                X1, X2, X3, X4, X5, X6 = X[xs]
                tX1, tX2, tX3, tX4, tX5, tX6 = t_X[xs]
                X4, tX4 = X4s[u % 3], t_X4s[u % 3]
                if blk == 0:
                    def pairs(ap, hd=hd):
                        wv = w_in[l].rearrange("(k p) c -> p k c", p=128)
                        a3 = ap[:, 0:KC * 384].rearrange("p (k a c) -> p k a c", a=3, c=128)
                        src2 = wv[:, :, 3072 + 128 * hd:3072 + 128 * hd + 1024].rearrange("p k (a b) -> p k a b", b=512)[:, :, :, 0:128]
                        return [(a3[:, :, 0:2, :], src2),
                                (a3[:, :, 2, :], wv[:, :, 4608 + 128 * hd:4608 + 128 * hd + 128])]
                    slab, t_sl_ = ring.load(pairs)
                    head_slab[hd] = (slab[:, 0:KC * 384].rearrange("p (k a c) -> p k a c", a=3, c=128), t_sl_)
                sl4, t_sl = head_slab[hd]

                def partA():
                    projT(0, lambda kc: sl4[:, kc, 0, :], hnT, t_hn[blk], t_sl, blk)
                    P.op("act", f_act(X1, banks[0][:, :], AF.Silu), reads=[t_bank[0]], writes=[tX1])
                    projT(1, lambda kc: sl4[:, kc, 1, :], hnT, t_hn[blk], t_sl, blk)
                    P.op("act", f_act(X2, banks[1][:, :], AF.Tanh, scale=0.5), reads=[t_bank[1]], writes=[tX2])

                def partB():
                    P.op("dve", f_ts(X3, X2, oml_c(hd), lb_c(hd), ALU.mult, ALU.add), reads=[tX2, t_small], writes=[tX3])
                    P.op("dve", f_ts(X2, X2, noml_c(hd), oml_c(hd), ALU.mult, ALU.add), reads=[tX2, t_small], writes=[tX2])
                    P.op("act", f_act(X3, X3, AF.Ln), reads=[tX3], writes=[tX3])

                def partC():
                    P.op("dve", lambda e, X4=X4, X3=X3: e.tensor_tensor_scan(out=X4, data0=rstm, data1=X3, initial=0.0,
                                                                             op0=ALU.mult, op1=ALU.add),
                         reads=[tX3, t_cst], writes=[tX4])
                    b3 = X4.rearrange("p (n c) -> p n c", c=32)
                    P.op("dve", f_tt(X3.rearrange("p (n c) -> p n c", c=32),
                                     X4[:, 31:TB:32].unsqueeze(2).to_broadcast([128, 16, 32]), b3, ALU.subtract),
                         reads=[tX4], writes=[tX3])
                    P.op("act", f_act(X5, X4, AF.Exp), reads=[tX4], writes=[tX5])
                    P.op("act", f_act(X6, X4, AF.Exp, scale=-1.0), reads=[tX4], writes=[tX6])
                    P.op("act", f_act(X3, X3, AF.Exp), reads=[tX3], writes=[tX3])

                def partD():
                    k2 = u % 2
                    P.op("dve", f_tt(KD[k2], X2, X3, ALU.mult), reads=[tX2, tX3], writes=[t_KD[k2]])
                    pbt = banks[3][:].bitcast(BF16)
                    P.op("pe", f_trs([(pbt[:, i * 128:(i + 1) * 128], KD[k2][:, i * 128:(i + 1) * 128], ident) for i in range(4)]),
                         reads=[t_KD[k2], t_cst], writes=[t_bank[3]])
                    P.op("act", f_acopy(kd_tok[k2], pbt[:, 0:512].rearrange("p (i c) -> p i c", c=128)),
                         reads=[t_bank[3]], writes=[t_kt[k2]])
                    P.op("dve", f_tt(QE[xs], X1, X5, ALU.mult), reads=[tX1, tX5], writes=[t_QE[xs]])
                    P.op("dve", f_tt(KE[xs], X2, X6, ALU.mult), reads=[tX2, tX6], writes=[t_KE[xs]])
                return [partA, partB, partC, partD]

            F32R = mybir.dt.float32r
            USE_R = False
            r32 = (lambda ap: ap.bitcast(F32R)) if USE_R else (lambda ap: ap)
            tctr = {"n": 0}

            def unit_ctx(g):
                u, i = divmod(g, 4)
                hd, blk = units[u]
                xs = u % NX
                tile = blk * 4 + i
                return u, i, hd, blk, xs, tile, tile % NV, (4 if g % 2 == 0 else 2)

            def st_V(g):
                u, i, hd, blk, xs, tile, vi, ub = unit_ctx(g)
                vh = v_tok[:, tile, hd * 128:(hd + 1) * 128]
                P.op("dve", f_tt(vbd[vi].rearrange("p (n c) -> p n c", c=128),
                                 vh.unsqueeze(1).to_broadcast([128, 4, 128]),
                                 vblk.rearrange("p (n c) -> p n c", c=128), ALU.mult),
                     reads=[t_v, t_cst], writes=[t_vbd[vi]])

            def st_UA(g):
                u, i, hd, blk, xs, tile, vi, ub = unit_ctx(g)
                P.op("pe", f_mms([(banks[ub][:, :], kd_tok[u % 2][:, i, :], vbd[vi], True, True, {})]),
                     reads=[t_kt[u % 2], t_vbd[vi]], writes=[t_bank[ub]])
                P.op("pe", f_mms([(banks[5][:, 0:128], KE[xs][:, i * 128:(i + 1) * 128], QE[xs][:, i * 128:(i + 1) * 128], True, True, {})]),
                     reads=[t_KE[xs], t_QE[xs]], writes=[t_bank[5]])

            def st_M(g):
                u, i, hd, blk, xs, tile, vi, ub = unit_ctx(g)
                P.op("dve", f_tt(attm[vi], banks[5][:, 0:128], blk01, ALU.mult),
                     reads=[t_bank[5], t_cst], writes=[t_att[vi]])

            def st_S(g):
                u, i, hd, blk, xs, tile, vi, ub = unit_ctx(g)
                X5, tX5 = X[xs][4], t_X[xs][4]
                sidx = 4 * g
                if blk == 0 and i == 0:
                    P.op("dve", f_memset(Sst[sidx % NS], 0.0), writes=[t_S[sidx % NS]])
                half = g % 2
                for n_ in range(4):
                    cs = i * 128 + n_ * 32
                    s_cur = (sidx + n_) % NS
                    s_nxt = (sidx + n_ + 1) % NS
                    if n_ == 3:
                        P.op("act", f_acopy(S_bf[half], Sring[:, half * 512:(half + 1) * 512]),
                             reads=[t_S[(sidx + k_) % NS] for k_ in range(4)], writes=[t_Sb[half]])
                    P.op("dve", f_stt(Sst[s_nxt], Sst[s_cur], X5[:, cs + 31:cs + 32], banks[ub][:, n_ * 128:(n_ + 1) * 128],
                                      ALU.mult, ALU.add),
                         reads=[t_S[s_cur], tX5, t_bank[ub]], writes=[t_S[s_nxt]])

            def st_O(g):
                u, i, hd, blk, xs, tile, vi, ub = unit_ctx(g)
                half = g % 2
                vh = v_tok[:, tile, hd * 128:(hd + 1) * 128]
                items = [(banks[6][:, i * 128:(i + 1) * 128], vh, attm[vi], i == 0, False, {"skip_group_check": True})]
                rds = [t_v, t_att[vi], t_QE[xs], t_Sb[half]]
                for n_ in range(4):
                    cs = i * 128 + n_ * 32
                    items.append((banks[6][:, cs:cs + 32], S_bf[half][:, n_ * 128:(n_ + 1) * 128], QE[xs][:, cs:cs + 32], False,
                                  (i == 3 and n_ == 3), {"skip_group_check": True}))
                P.op("pe", f_mms(items), reads=rds, writes=[t_bank[6]])

            def S2_end1(u):
                X4, tX4 = X4s[u % 3], t_X4s[u % 3]
                P.op("act", f_acopy(X4, banks[6][:, :]), reads=[t_bank[6]], writes=[tX4])
                P.op("act", f_act(sq1, X4.rearrange("p (c t) -> p c t", c=1), AF.Square), reads=[tX4], writes=[t_sq1])

            def S2_end2(u):
                hd, blk = units[u]
                X4, tX4 = X4s[u % 3], t_X4s[u % 3]
                tsl = slice(blk * TB, (blk + 1) * TB)
                P.op("pe", f_mms([(banks[7][:, :], ones_bf, sq1[:, 0, :], True, True, {})]), reads=[t_sq1, t_cst], writes=[t_bank[7]])
                P.op("act", f_act(rs1, banks[7][:, :], AF.Ln, bias=eps_ap, scale=1.0 / 128), reads=[t_bank[7], t_small], writes=[t_r1])
                P.op("act", f_act(rs1, rs1, AF.Exp, scale=-0.5), reads=[t_r1], writes=[t_r1])
                P.op("dve", f_tt(X4, X4, rs1, ALU.mult), reads=[tX4, t_r1], writes=[tX4])
                sl4g, t_slg = head_slab[hd]
                projT(1, lambda kc: sl4g[:, kc, 2, :], hnT, t_hn[blk], t_slg, blk)
                P.op("act", f_act(X7, banks[1][:, :], AF.Silu), reads=[t_bank[1]], writes=[t_X7])
                P.op("dve", f_stt(ocT[:, hd, tsl], X4, vec[:, vb + V_OG:vb + V_OG + 1], X7, ALU.mult, ALU.mult),
                     reads=[tX4, t_X7, t_vec], writes=[t_oc[blk]])

            NU_ = len(units)
            parts = {}

            def part(u, k):
                if u >= NU_:
                    return
                if u not in parts:
                    parts[u] = S1_parts(u)
                parts[u][k]()

            for k in range(4):
                part(0, k)
            part(1, 0)
            part(1, 1)
            pend = None
            NG = 4 * NU_
            st_V(0)
            st_UA(0)
            st_M(0)
            st_V(1)
            for g in range(NG):
                u, i = divmod(g, 4)
                if i == 3:
                    part(u + 1, 3)
                st_S(g)
                if g + 1 < NG:
                    st_UA(g + 1)
                st_O(g)
                if g + 2 < NG:
                    st_V(g + 2)
                if g + 1 < NG:
                    st_M(g + 1)
                if i == 0:
                    if pend is not None:
                        S2_end2(pend)
                        pend = None
                    part(u + 2, 0)
                elif i == 1:
                    part(u + 1, 2)
                elif i == 2:
                    part(u + 2, 1)
                else:
                    S2_end1(u)
                    pend = u
            S2_end2(pend)
            if l == 0:
                dbg_dump("oc0", ocT, t_oc[3], [128, 4, S])

            P.barrier(["pe", "act", "dve"] + (["pool", "sp"] if dbg else []), dma=bool(dbg))
            AR.off = 56 * 1024
            o_aq = AR.alloc(2 * S * 2)
            o_ak = AR.alloc(4 * S * 2)
            aqT = AR.bf(o_aq, 2 * S).rearrange("p (c t) -> p c t", t=S)
            akz = AR.bf(o_ak, 4 * S).rearrange("p (c h t) -> p c h t", h=2, t=S)
            P.op("dve", f_memset(AR.bf(o_ak, 4 * S), 0.0), writes=[t_ak_all := T("akz")])
            t_aq = [T("aq%d" % i) for i in range(NB)]
            t_ak = [T("ak%d" % i) for i in range(NB)]
            o_av = AR.alloc(NT * 4 * 65 * 2)
            av = AR.bf(o_av, NT * 260).rearrange("p (t h d) -> p t h d", h=4, d=65)
            t_av = T("av")
            o_oacc = AR.alloc(NT * 260 * 4)
            oacc = AR.f32(o_oacc, NT * 260).rearrange("p (t c) -> p t c", c=260)
            t_oacc = [T("oacc%d" % i) for i in range(NT)]
            NPT = 6
            o_pt = [AR.alloc(512 * 2) for _ in range(NPT)]
            PT = [AR.bf(o, 512) for o in o_pt]
            t_pt = [T("pt%d" % i) for i in range(NPT)]
            o_otk = [AR.alloc(512 * 2) for _ in range(2)]
            otk = [AR.bf(o, 512) for o in o_otk]
            t_otk = [T("otk%d" % i) for i in range(2)]
            o_rd = AR.alloc(64)
            rden = AR.f32(o_rd, 8)
            t_rd = T("rden")

            class AttnPipe:
                def __init__(self, PT, t_pt, sbanks):
                    self.PT, self.t_pt = PT, t_pt
                    self.sbanks = sbanks
                    self.LA = min(len(sbanks), len(PT)) - 1
                    self.q = []
                    self.rot = 0

                def add(self, chunk, obank, first, post):
                    sbk = self.sbanks[self.rot % len(self.sbanks)]
                    pti = self.rot % len(self.PT)
                    self.rot += 1
                    PTb, tptb = self.PT[pti], self.t_pt[pti]
                    n = len(chunk) * 128
                    mi0 = chunk[0][2]
                    assert all(c_[2] == mi0 for c_ in chunk)
                    items = []
                    rds = []
                    for s_, (kT, qT, mi, vap, ocol, r_) in enumerate(chunk):
                        items.append((banks[sbk][:, s_ * 128:(s_ + 1) * 128], kT, qT, True, True, {}))
                        rds += r_
                    P.op("pe", f_mms(items), reads=rds, writes=[t_bank[sbk]])
                    P.op("act", f_act(PTb[:, 0:n], banks[sbk][:, 0:n], AF.Exp, scale=0.125),
                         reads=[t_bank[sbk]], writes=[tptb])
                    P.op("dve", f_tt(PTb[:, 0:n].rearrange("p (a c) -> p a c", c=128),
                                      PTb[:, 0:n].rearrange("p (a c) -> p a c", c=128),
                                      mask_ap(mi0).unsqueeze(1).to_broadcast([128, len(chunk), 128]), ALU.mult),
                         reads=[tptb, t_cst], writes=[tptb])
                    self.q.append((chunk, obank, first, post, PTb, tptb))
                    while len(self.q) > self.LA:
                        self._pv()

                def _pv(self):
                    chunk, obank, first, post, PTb, tptb = self.q.pop(0)
                    items = []
                    rds = [tptb]
                    for s_, (kT, qT, mi, vap, ocol, r_) in enumerate(chunk):
                        items.append((banks[obank][:, ocol:ocol + 65], PTb[:, s_ * 128:(s_ + 1) * 128], vap, first and s_ == 0, False,
                                      {"skip_group_check": True}))
                        rds += r_
                    P.op("pe", f_mms(items), reads=rds, writes=[t_bank[obank]])
                    if post is not None:
                        post()

                def flush(self):
                    while self.q:
                        self._pv()

            def attention(pipe, combos, obank, post):
                nb_ = (len(combos) + 3) // 4
                for bi in range(nb_):
                    pipe.add(combos[bi * 4:bi * 4 + 4], obank, bi == 0, post if bi == nb_ - 1 else None)

            pipe = AttnPipe(PT, t_pt, [2, 3, 4, 0, 1, 7])
            for g in range(3):
                def pairs(ap, g=g):
                    wv = w_in[l].rearrange("(k p) c -> p k c", p=128)
                    src = wv[:, :, 0:1536].rearrange("p k (a b) -> p k a b", b=768)[:, :, :, 256 * g:256 * g + 256]
                    return [(ap[:, 0:KC * 512].rearrange("p (k a c) -> p k a c", a=2, c=256), src)]
                slab, t_sl = ring.load(pairs)
                sl4 = slab[:, 0:KC * 512].rearrange("p (k a c) -> p k a c", a=2, c=256)
                slab_v, t_slv = ring.load(lambda ap, g=g: [(ap[:, 0:KC * 256].rearrange("p (k c) -> p k c", c=256),
                                                            w_in[l].rearrange("(k p) c -> p k c", p=128)[:, :, 1536 + 256 * g:1536 + 256 * g + 256])])
                slv3 = slab_v[:, 0:KC * 256].rearrange("p (k c) -> p k c", c=256)
                for tb in range(NB):
                    tsl = slice(tb * TB, (tb + 1) * TB)
                    for j in range(2):
                        bk_ = (2 * tb + j) % 2
                        projT(bk_, lambda kc, j=j: sl4[:, kc, 0, j * 128:(j + 1) * 128], hnT, t_hn[tb], t_sl, tb)
                        P.op("act", f_acopy(aqT[:, j, tsl], banks[bk_][:, :]), reads=[t_bank[bk_]], writes=[t_aq[tb]])
                    for j in range(2):
                        bk_ = (2 * tb + j) % 2
                        projT(bk_, lambda kc, j=j: sl4[:, kc, 1, j * 128:(j + 1) * 128], hnT, t_hn[tb], t_sl, tb)
                        P.op("act", f_acopy(akz[0:64, j, 0, tsl], banks[bk_][0:64, :]), reads=[t_bank[bk_], t_ak_all], writes=[t_ak[tb]])
                        P.op("act", f_acopy(akz[64:128, j, 1, tsl], banks[bk_][64:128, :]), reads=[t_bank[bk_], t_ak_all], writes=[t_ak[tb]])
                P.op("dve", f_memset(av.rearrange("p t h d -> p (t h) d")[:, :, 64:65], 1.0), writes=[t_av])
                for tile in range(NT):
                    bk_ = tile % 2
                    items = [(banks[bk_][:, 0:256], hnT[:, kc, tile * 128:(tile + 1) * 128], slv3[:, kc, :], kc == 0, kc == KC - 1, {})
                             for kc in range(KC)]
                    P.op("pe", f_mms(items), reads=[t_slv, t_hn[tile // 4]], writes=[t_bank[bk_]])
                    P.op("act", f_acopy(av[:, tile, :, 0:64], banks[bk_][:, 0:256].rearrange("p (h d) -> p h d", d=64)),
                         reads=[t_bank[bk_]], writes=[t_av])
                for T_ in range(NT):
                    if g == 0:
                        Js = [(T_, M_CAUSAL)] + ([(T_ - 1, M_OFFA)] if T_ >= 1 else [])
                    elif g == 1:
                        Js = [(T_, M_4C)] + [(T_ - d_, M_4) for d_ in (1, 2, 3) if T_ - d_ >= 0] + ([(T_ - 4, M_4U)] if T_ >= 4 else [])
                    else:
                        Js = [(T_, M_16C)] + [(J, M_16) for J in range(T_)]
                    combos = []
                    for (J, mi) in Js:
                        for hh in range(4):
                            j, half = hh // 2, hh % 2
                            combos.append((akz[:, j, half, J * 128:(J + 1) * 128], aqT[:, j, T_ * 128:(T_ + 1) * 128], mi,
                                           av[:, J, hh, :], hh * 65, [t_ak[J // 4], t_aq[T_ // 4], t_av]))
                    ob_ = 5 + (T_ % 2)

                    def post(T_=T_, ob_=ob_, g=g):
                        if g == 0:
                            P.op("dve", f_copy(oacc[:, T_, :], banks[ob_][:, 0:260]), reads=[t_bank[ob_]], writes=[t_oacc[T_]])
                        else:
                            P.op("dve", f_tt(oacc[:, T_, :], banks[ob_][:, 0:260], oacc[:, T_, :], ALU.add),
                                 reads=[t_bank[ob_], t_oacc[T_]], writes=[t_oacc[T_]])
                    attention(pipe, combos, ob_, post)
                pipe.flush()
            for T_ in range(NT):
                o3 = oacc[:, T_, :].rearrange("p (h d) -> p h d", d=65)
                oi = T_ % 2
                P.op("dve", lambda e, o3=o3, rd=rden: e.reciprocal(out=rd[:, 0:4], in_=o3[:, :, 64]), reads=[t_oacc[T_]], writes=[t_rd])
                P.op("dve", f_tt(otk[oi][:, 0:256].rearrange("p (h d) -> p h d", d=64), o3[:, :, 0:64],
                                 rden[:, 0:4].unsqueeze(2).to_broadcast([128, 4, 64]), ALU.mult),
                     reads=[t_oacc[T_], t_rd], writes=[t_otk[oi]])
                pbt = banks[7][:].bitcast(BF16)
                P.op("pe", f_trs([(pbt[:, i * 128:(i + 1) * 128], otk[oi][:, i * 128:(i + 1) * 128], ident) for i in range(2)]),
                     reads=[t_otk[oi], t_cst], writes=[t_bank[7]])
                P.op("act", f_acopy(oaT[:, :, T_ * 128:(T_ + 1) * 128], pbt[:, 0:256].rearrange("p (i c) -> p i c", c=128)),
                     reads=[t_bank[7]], writes=[t_oa[T_ // 4]])
            if l == 0:
                dbg_dump("oa0", oaT, t_oa[3], [128, 2, S])

            P.barrier(["pe", "act", "dve"] + (["pool", "sp"] if dbg else []), dma=bool(dbg))
            AR.off = base_mark
            o_bq = AR.alloc(4 * S * 2)
            bqT = AR.bf(o_bq, 4 * S).rearrange("p (c t) -> p c t", t=S)
            o_bk = AR.alloc(4 * S * 2)
            bkz = AR.bf(o_bk, 4 * S).rearrange("p (k h t) -> p k h t", h=2, t=S)
            P.op("dve", f_memset(AR.bf(o_bk, 4 * S), 0.0), writes=[t_bk_all := T("bkz")])
            t_bq = [T("bq%d" % i) for i in range(NB)]
            t_bk = [T("bk%d" % i) for i in range(NB)]
            o_bv = AR.alloc(NT * 2 * 65 * 2)
            bv = AR.bf(o_bv, NT * 130).rearrange("p (t h d) -> p t h d", h=2, d=65)
            t_bv = T("bv")
            o_pt = [AR.alloc(512 * 2) for _ in range(5)]
            PT = [AR.bf(o, 512) for o in o_pt]
            t_pt = [T("ptb%d" % i) for i in range(5)]
            o_otk = [AR.alloc(512 * 2) for _ in range(2)]
            otk = [AR.bf(o, 512) for o in o_otk]
            t_otk = [T("otkb%d" % i) for i in range(2)]
            o_rd = AR.alloc(64)
            rden = AR.f32(o_rd, 8)
            t_rd = T("rdenb")

            slab_q, t_slq = ring.load(lambda ap: [(ap[:, 0:KC * 512].rearrange("p (k c) -> p k c", c=512),
                                                   w_in[l].rearrange("(k p) c -> p k c", p=128)[:, :, 2304:2816])])
            sq3 = slab_q[:, 0:KC * 512].rearrange("p (k c) -> p k c", c=512)

            def pairs(ap):
                wv = w_in[l].rearrange("(k p) c -> p k c", p=128)
                a3 = ap[:, 0:KC * 384].rearrange("p (k c) -> p k c", c=384)
                return [(a3[:, :, 0:256], wv[:, :, 2816:3072]),
                        (a3[:, :, 256:320], wv[:, :, 2880:2944]),
                        (a3[:, :, 320:384], wv[:, :, 2816:2880])]
            slab_k, t_slk = ring.load(pairs)
            sk3 = slab_k[:, 0:KC * 384].rearrange("p (k c) -> p k c", c=384)
            for tb in range(NB):
                tsl = slice(tb * TB, (tb + 1) * TB)
                for j in range(4):
                    bk_ = j % 2
                    projT(bk_, lambda kc, j=j: sq3[:, kc, j * 128:(j + 1) * 128], hnT, t_hn[tb], t_slq, tb)
                    P.op("act", f_acopy(bqT[:, j, tsl], banks[bk_][:, :]), reads=[t_bank[bk_]], writes=[t_bq[tb]])
                for v_ in range(2):
                    bk_ = v_ % 2
                    c0 = 0 if v_ == 0 else 256
                    projT(bk_, lambda kc, c0=c0: sk3[:, kc, c0:c0 + 128], hnT, t_hn[tb], t_slk, tb)
                    kv_lo, kv_hi = (0, 1) if v_ == 0 else (1, 0)
                    P.op("act", f_acopy(bkz[0:64, kv_lo, 0, tsl], banks[bk_][0:64, :]), reads=[t_bank[bk_], t_bk_all], writes=[t_bk[tb]])
                    P.op("act", f_acopy(bkz[64:128, kv_hi, 1, tsl], banks[bk_][64:128, :]), reads=[t_bank[bk_], t_bk_all], writes=[t_bk[tb]])
            P.op("dve", f_memset(bv.rearrange("p t h d -> p (t h) d")[:, :, 64:65], 1.0), writes=[t_bv])
            for tile in range(NT):
                bk_ = tile % 2
                items = [(banks[bk_][:, 0:128], hnT[:, kc, tile * 128:(tile + 1) * 128], sk3[:, kc, 128:256], kc == 0, kc == KC - 1, {})
                         for kc in range(KC)]
                P.op("pe", f_mms(items), reads=[t_slk, t_hn[tile // 4]], writes=[t_bank[bk_]])
                P.op("act", f_acopy(bv[:, tile, :, 0:64], banks[bk_][:, 0:128].rearrange("p (h d) -> p h d", d=64)),
                     reads=[t_bank[bk_]], writes=[t_bv])
            esk = small[:, 32 + 8 * l:40 + 8 * l]
            pipe = AttnPipe(PT, t_pt, [2, 3, 4, 0, 1])
            for T_ in range(NT):
                Js = [(T_, M_CAUSAL)] + ([(T_ - 1, M_OFFB)] if T_ >= 1 else [])
                oi = T_ % 2
                for hq in range(2):
                    combos = []
                    for (J, mi) in Js:
                        for h4 in range(4):
                            h = hq * 4 + h4
                            j, half, kv = h // 2, h % 2, h // 4
                            combos.append((bkz[:, kv, half, J * 128:(J + 1) * 128], bqT[:, j, T_ * 128:(T_ + 1) * 128], mi,
                                           bv[:, J, kv, :], h4 * 65, [t_bk[J // 4], t_bq[T_ // 4], t_bv]))
                    ob_ = 5 + hq

                    def post(T_=T_, ob_=ob_, hq=hq, oi=oi, rden=rden, otk=otk, t_otk=t_otk, t_rd=t_rd):
                        o3 = banks[ob_][:, 0:260].rearrange("p (h d) -> p h d", d=65)
                        P.op("dve", f_tt(rden[:, 4 * hq:4 * hq + 4], o3[:, :, 64], esk[:, hq * 4:hq * 4 + 4], ALU.add),
                             reads=[t_bank[ob_], t_small], writes=[t_rd])
                        P.op("dve", lambda e, rd=rden[:, 4 * hq:4 * hq + 4]: e.reciprocal(out=rd, in_=rd), reads=[t_rd], writes=[t_rd])
                        P.op("dve", f_tt(otk[oi][:, hq * 256:(hq + 1) * 256].rearrange("p (h d) -> p h d", d=64), o3[:, :, 0:64],
                                         rden[:, 4 * hq:4 * hq + 4].unsqueeze(2).to_broadcast([128, 4, 64]), ALU.mult),
                             reads=[t_bank[ob_], t_rd], writes=[t_otk[oi]])
                        if hq == 1:
                            pbt = banks[7][:].bitcast(BF16)
                            P.op("pe", f_trs([(pbt[:, i * 128:(i + 1) * 128], otk[oi][:, i * 128:(i + 1) * 128], ident) for i in range(4)]),
                                 reads=[t_otk[oi], t_cst], writes=[t_bank[7]])
                            P.op("act", f_acopy(obT[:, :, T_ * 128:(T_ + 1) * 128], pbt[:, 0:512].rearrange("p (i c) -> p i c", c=128)),
                                 reads=[t_bank[7]], writes=[t_ob[T_ // 4]])
                    attention(pipe, combos, ob_, post)
            pipe.flush()
            if l == 0:
                dbg_dump("ob0", obT, t_ob[3], [128, 4, S])

            P.barrier(["pe", "act", "dve"] + (["pool", "sp"] if dbg else []), dma=bool(dbg))
            AR.off = base_mark
            o_mx = AR.alloc(KC * S * 2)
            mixT = AR.bf(o_mx, KC * S).rearrange("p (c t) -> p c t", t=S)
            t_mx = [T("mixT%d" % i) for i in range(NB)]
            o_sg = [AR.alloc(TB * 4) for _ in range(3)]
            sg = [AR.f32(o, TB) for o in o_sg]
            t_sg = [T("sg%d" % i) for i in range(3)]
            for j in range(KC):
                def pairs(ap, j=j):
                    wv = w_in[l].rearrange("(k p) c -> p k c", p=128)
                    g4 = ap[:, 0:KC * 384].rearrange("p (k a c) -> p k a c", a=3, c=128)
                    src = wv[:, :, 5120:8192].rearrange("p k (a b) -> p k a b", b=1024)[:, :, :, j * 128:(j + 1) * 128]
                    o0 = KC * 384
                    wa3 = ap[:, o0:o0 + 256].rearrange("p (k c) -> p k c", c=128)
                    wb3 = ap[:, o0 + 256:o0 + 768].rearrange("p (k c) -> p k c", c=128)
                    wc3 = ap[:, o0 + 768:o0 + 1280].rearrange("p (k c) -> p k c", c=128)
                    return [(g4, src),
                            (wa3, w_a[l].rearrange("(k p) c -> p k c", p=128)[:, :, j * 128:(j + 1) * 128]),
                            (wb3, w_b[l].rearrange("(k p) c -> p k c", p=128)[:, :, j * 128:(j + 1) * 128]),
                            (wc3, w_c[l].rearrange("(k p) c -> p k c", p=128)[:, :, j * 128:(j + 1) * 128])]
                slab, t_sl = ring.load(pairs)
                g4 = slab[:, 0:KC * 384].rearrange("p (k a c) -> p k a c", a=3, c=128)
                o0 = KC * 384
                wa3 = slab[:, o0:o0 + 256].rearrange("p (k c) -> p k c", c=128)
                wb3 = slab[:, o0 + 256:o0 + 768].rearrange("p (k c) -> p k c", c=128)
                wc3 = slab[:, o0 + 768:o0 + 1280].rearrange("p (k c) -> p k c", c=128)
                for tb in range(NB):
                    tsl = slice(tb * TB, (tb + 1) * TB)
                    for i in range(3):
                        projT(i, lambda kc, i=i: g4[:, kc, i, :], hnT, t_hn[tb], t_sl, tb)
                        P.op("act", f_act(sg[i], banks[i][:, :], AF.Sigmoid), reads=[t_bank[i]], writes=[t_sg[i]])
                    for i, (w3, nk, oT_, t_o) in enumerate([(wa3, 2, oaT, t_oa), (wb3, 4, obT, t_ob), (wc3, 4, ocT, t_oc)]):
                        items = [(banks[3 + i][:, :], w3[:, kc, :], oT_[:, kc, tsl], kc == 0, kc == nk - 1, {}) for kc in range(nk)]
                        P.op("pe", f_mms(items), reads=[t_sl, t_o[tb]], writes=[t_bank[3 + i]])
                    P.op("dve", f_tt(sg[0], sg[0], banks[3][:, :], ALU.mult), reads=[t_sg[0], t_bank[3]], writes=[t_sg[0]])
                    P.op("dve", f_tt(sg[1], sg[1], banks[4][:, :], ALU.mult), reads=[t_sg[1], t_bank[4]], writes=[t_sg[1]])
                    P.op("dve", f_tt(sg[0], sg[0], sg[1], ALU.add), reads=[t_sg[0], t_sg[1]], writes=[t_sg[0]])
                    P.op("dve", f_tt(sg[2], sg[2], banks[5][:, :], ALU.mult), reads=[t_sg[2], t_bank[5]], writes=[t_sg[2]])
                    P.op("dve", f_tt(mixT[:, j, tsl], sg[0], sg[2], ALU.add), reads=[t_sg[0], t_sg[2]], writes=[t_mx[tb]])

            P.barrier(["pe", "act", "dve"] + (["pool", "sp"] if dbg else []), dma=bool(dbg))
            AR.off = 0
            o_mo = [AR.alloc(KC * TB * 4) for _ in range(2)]
            moT2 = [AR.f32(o, KC * TB).rearrange("p (c t) -> p c t", t=TB) for o in o_mo]
            t_mo2 = [T("moT%d" % i) for i in range(2)]
            o_sq = [AR.alloc(KC * TB * 2) for _ in range(2)]
            sq2 = [AR.bf(o, KC * TB).rearrange("p (c t) -> p c t", t=TB) for o in o_sq]
            t_sq2 = [T("sqg%d" % i) for i in range(2)]
            o_rs = AR.alloc(TB * 4)
            rs_t = AR.f32(o_rs, TB)
            t_r = T("rstdg")
            assert AR.off <= base_mark
            wo_slabs = []
            for jh in range(2):
                slab, t_sl = ring.load(lambda ap, jh=jh: [(ap[:, 0:KC * 512].rearrange("p (k c) -> p k c", c=512),
                                                           w_o[l].rearrange("(k p) c -> p k c", p=128)[:, :, jh * 512:(jh + 1) * 512])])
                wo_slabs.append((slab[:, 0:KC * 512].rearrange("p (k c) -> p k c", c=512), t_sl))

            def g2_rest(tb):
                tsl = slice(tb * TB, (tb + 1) * TB)
                mo, tmo = moT2[tb % 2], t_mo2[tb % 2]
                sq_, tsq_ = sq2[tb % 2], t_sq2[tb % 2]
                items = [(banks[5][:, :], ones_bf, sq_[:, c, :], c == 0, c == KC - 1, {}) for c in range(KC)]
                P.op("pe", f_mms(items), reads=[tsq_, t_cst], writes=[t_bank[5]])
                P.op("act", f_act(rs_t, banks[5][:, :], AF.Ln, bias=eps_ap, scale=1.0 / D), reads=[t_bank[5], t_small], writes=[t_r])
                P.op("act", f_act(rs_t, rs_t, AF.Exp, scale=-0.5), reads=[t_r], writes=[t_r])
                P.op("dve", f_tt(mo, mo, rs_t.unsqueeze(1).to_broadcast([128, KC, TB]), ALU.mult), reads=[tmo, t_r], writes=[tmo])
                for c in range(KC):
                    P.op("dve", f_stt(hT[:, c, tsl], mo[:, c, :], vec[:, vb + V_NPO + c:vb + V_NPO + c + 1], hT[:, c, tsl],
                                      ALU.mult, ALU.add), reads=[tmo, t_vec, t_h[tb]], writes=[t_h[tb]])

            pend = None
            for tb in range(NB):
                tsl = slice(tb * TB, (tb + 1) * TB)
                mo, tmo = moT2[tb % 2], t_mo2[tb % 2]
                for j in range(KC):
                    s3, t_sl = wo_slabs[j // 4]
                    jj = j % 4
                    bk_ = 6 + (j % 2)
                    items = [(banks[bk_][:, :], s3[:, kc, jj * 128:(jj + 1) * 128], mixT[:, kc, tsl], kc == 0, kc == KC - 1, {})
                             for kc in range(KC)]
                    P.op("pe", f_mms(items), reads=[t_sl, t_mx[tb]], writes=[t_bank[bk_]])
                    P.op("act", f_acopy(mo[:, j, :], banks[bk_][:, :]), reads=[t_bank[bk_]], writes=[tmo])
                    if j == 3 and pend is not None:
                        g2_rest(pend)
                        pend = None
                if l == 0 and tb == 0:
                    dbg_dump("mixo0", mo, tmo, [128, KC, TB])
                P.op("act", f_act(sq2[tb % 2], mo, AF.Square), reads=[tmo], writes=[t_sq2[tb % 2]])
                pend = tb
            g2_rest(pend)
            if l == 0:
                dbg_dump("hmid0", hT[:], t_h[3], [128, KC, S])

            P.barrier(["pe", "act", "dve", "pool"] + (["sp"] if dbg else []))
            AR.off = 0
            o_xs = [AR.alloc(SLOT_EL * 2) for _ in range(2)]
            ring_f = Ring([s_[:] for s_ in slots] + [AR.bf(o, SLOT_EL) for o in o_xs])
            o_hn2 = AR.alloc(KC * TB * 2)
            hn2 = AR.bf(o_hn2, KC * TB).rearrange("p (c t) -> p c t", t=TB)
            t_hn2 = T("hn2")
            o_g = AR.alloc(32 * TB * 2)
            gT = AR.bf(o_g, 32 * TB).rearrange("p (c t) -> p c t", t=TB)
            t_g = [T("g%d" % i) for i in range(4)]
            o_ff = AR.alloc(KC * TB * 4)
            ffT = AR.f32(o_ff, KC * TB).rearrange("p (c t) -> p c t", t=TB)
            t_ff = T("ffT")
            NU = 3
            o_U = [[AR.alloc((TB + 2) * 4) for _ in range(2)] for _ in range(NU)]
            U = [[AR.f32(o, TB + 2) for o in os_] for os_ in o_U]
            t_U = [[T("U%d_%d" % (a_, b_)) for b_ in range(2)] for a_ in range(NU)]
            o_Y = [[AR.alloc(TB * 4) for _ in range(2)] for _ in range(NU)]
            Y = [[AR.f32(o, TB) for o in os_] for os_ in o_Y]
            t_Y = [[T("Y%d_%d" % (a_, b_)) for b_ in range(2)] for a_ in range(NU)]
            o_cr = AR.alloc(64 * 2 * 4)
            carry = AR.f32(o_cr, 128).rearrange("p (c t) -> p c t", t=2)
            t_cr = T("carry")
            t_crs = [T("carry%d" % i) for i in range(64)]
            o_sq = AR.alloc(KC * TB * 2)
            sq = AR.bf(o_sq, KC * TB).rearrange("p (c t) -> p c t", t=TB)
            t_sq = T("sqf")
            o_rs = [AR.alloc(TB * 4) for _ in range(2)]
            rs_pre, rs_post = AR.f32(o_rs[0], TB), AR.f32(o_rs[1], TB)
            t_rpre, t_rpost = T("rpre"), T("rpost")
            P.op("dve", f_memset(carry, 0.0), writes=t_crs)
            cw = lambda j, c: vec[:, vb + V_CW + j * 64 + c:vb + V_CW + j * 64 + c + 1]
            cb = lambda c: vec[:, vb + V_CB + c:vb + V_CB + c + 1]
            A_BANKS, B_BANKS = [0, 1, 4], [2, 3, 5]

            def pre_norm_sq(tb):
                tsl = slice(tb * TB, (tb + 1) * TB)
                P.op("act", f_act(sq, hT[:, :, tsl], AF.Square), reads=[t_h[tb]], writes=[t_sq])

            def pre_norm_rest(tb, bank):
                tsl = slice(tb * TB, (tb + 1) * TB)
                items = [(banks[bank][:, :], ones_bf, sq[:, c, :], c == 0, c == KC - 1, {}) for c in range(KC)]
                P.op("pe", f_mms(items), reads=[t_sq, t_cst], writes=[t_bank[bank]])
                P.op("act", f_act(rs_pre, banks[bank][:, :], AF.Ln, bias=eps_ap, scale=1.0 / D),
                     reads=[t_bank[bank], t_small], writes=[t_rpre])
                P.op("act", f_act(rs_pre, rs_pre, AF.Exp, scale=-0.5), reads=[t_rpre], writes=[t_rpre])
                for c in range(KC):
                    P.op("dve", f_stt(hn2[:, c, :], hT[:, c, tsl], vec[:, vb + V_NPF + c:vb + V_NPF + c + 1],
                                      rs_pre, ALU.mult, ALU.mult), reads=[t_h[tb], t_rpre, t_vec], writes=[t_hn2])

            def post_norm_sq(tb):
                P.op("act", f_act(sq, ffT, AF.Square), reads=[t_ff], writes=[t_sq])

            def post_norm_rest(tb, bank):
                tsl = slice(tb * TB, (tb + 1) * TB)
                items = [(banks[bank][:, :], ones_bf, sq[:, c, :], c == 0, c == KC - 1, {}) for c in range(KC)]
                P.op("pe", f_mms(items), reads=[t_sq, t_cst], writes=[t_bank[bank]])
                P.op("act", f_act(rs_post, banks[bank][:, :], AF.Ln, bias=eps_ap, scale=1.0 / D),
                     reads=[t_bank[bank], t_small], writes=[t_rpost])
                P.op("act", f_act(rs_post, rs_post, AF.Exp, scale=-0.5), reads=[t_rpost], writes=[t_rpost])
                P.op("dve", f_tt(ffT, ffT, rs_post.unsqueeze(1).to_broadcast([128, KC, TB]), ALU.mult),
                     reads=[t_ff, t_rpost], writes=[t_ff])
                for c in range(KC):
                    P.op("dve", f_stt(hT[:, c, tsl], ffT[:, c, :], vec[:, vb + V_NPFF + c:vb + V_NPFF + c + 1], hT[:, c, tsl],
                                      ALU.mult, ALU.add), reads=[t_ff, t_vec, t_h[tb]], writes=[t_h[tb]])

            pi = 0
            prev_pair = None
            pre_norm_sq(0)
            pre_norm_rest(0, 6)
            pending_post = None
            for tb in range(NB):
                for s_ in range(8):
                    sla, t_sla = ring_f.load(lambda ap, s_=s_: [(ap[:, 0:KC * 512].rearrange("p (k c) -> p k c", c=512),
                                                                 w_up[l].rearrange("(k p) c -> p k c", p=128)[:, :, s_ * 512:(s_ + 1) * 512])])
                    slb, t_slb = ring_f.load(lambda ap, s_=s_: [(ap[:, 0:KC * 512].rearrange("p (k c) -> p k c", c=512),
                                                                 w_up[l].rearrange("(k p) c -> p k c", p=128)[:, :, DFF + s_ * 512:DFF + (s_ + 1) * 512])])
                    a3 = sla[:, 0:KC * 512].rearrange("p (k c) -> p k c", c=512)
                    b3 = slb[:, 0:KC * 512].rearrange("p (k c) -> p k c", c=512)
                    for i in range(4):
                        ca = s_ * 4 + i
                        u_ = pi % NU
                        pi += 1
                        cur = []
                        for ab, (w3, t_w, cc) in enumerate([(a3, t_sla, ca), (b3, t_slb, 32 + ca)]):
                            bk_ = (A_BANKS if ab == 0 else B_BANKS)[u_]
                            Ub, tU = U[u_][ab], t_U[u_][ab]
                            P.op("act", f_acopy(Ub[:, 0:2], carry[:, cc, :]), reads=[t_crs[cc]], writes=[tU])
                            items = [(banks[bk_][:, :], w3[:, kc, i * 128:(i + 1) * 128], hn2[:, kc, :], kc == 0, kc == KC - 1, {})
                                     for kc in range(KC)]
                            P.op("pe", f_mms(items), reads=[t_w, t_hn2], writes=[t_bank[bk_]])
                            cur.append((bk_, cc, Ub, Y[u_][ab], tU, t_Y[u_][ab]))
                        for (bk_, cc, Ub, Yb, tU, tY) in cur:
                            P.op("act", f_acopy(Ub[:, 2:TB + 2], banks[bk_][:, :]), reads=[t_bank[bk_]], writes=[tU])
                            P.op("act", f_act(Yb, banks[bk_][:, :], AF.Identity, bias=cb(cc), scale=cw(2, cc)),
                                 reads=[t_bank[bk_], t_vec], writes=[tY])
                            P.op("act", f_acopy(carry[:, cc, :], banks[bk_][:, TB - 2:TB]), reads=[t_bank[bk_]], writes=[t_crs[cc]])
                        if prev_pair is not None:
                            pu, pca = prev_pair
                            P.op("act", f_act(Y[pu][0], Y[pu][0], AF.Gelu_apprx_tanh), reads=[t_Y[pu][0]], writes=[t_Y[pu][0]])
                        for (bk_, cc, Ub, Yb, tU, tY) in cur:
                            P.op("dve", f_stt(Yb, Ub[:, 1:TB + 1], cw(1, cc), Yb, ALU.mult, ALU.add), reads=[tU, tY, t_vec], writes=[tY])
                            P.op("dve", f_stt(Yb, Ub[:, 0:TB], cw(0, cc), Yb, ALU.mult, ALU.add), reads=[tU, tY, t_vec], writes=[tY])
                        if prev_pair is not None:
                            pu, pca = prev_pair
                            P.op("pool", f_tt(gT[:, pca, :], Y[pu][0], Y[pu][1], ALU.mult), reads=[t_Y[pu][0], t_Y[pu][1]], writes=[t_g[pca // 8]])
                        prev_pair = (u_, ca)
                    if s_ == 0 and pending_post is not None:
                        post_norm_rest(pending_post, 7)
                        pending_post = None
                pu, pca = prev_pair
                P.op("act", f_act(Y[pu][0], Y[pu][0], AF.Gelu_apprx_tanh), reads=[t_Y[pu][0]], writes=[t_Y[pu][0]])
                P.op("pool", f_tt(gT[:, pca, :], Y[pu][0], Y[pu][1], ALU.mult), reads=[t_Y[pu][0], t_Y[pu][1]], writes=[t_g[pca // 8]])
                prev_pair = None
                if l == 0 and tb == 0:
                    dbg_dump("g0", gT, t_g[3], [128, 32, TB])
                if tb + 1 < NB:
                    pre_norm_sq(tb + 1)
                for jh in range(2):
                    for kq in range(4):
                        sld, t_sld = ring_f.load(lambda ap, jh=jh, kq=kq: [(ap[:, 0:KC * 512].rearrange("p (k c) -> p k c", c=512),
                                                                           w_dn[l].rearrange("(k p) c -> p k c", p=128)[:, kq * 8:(kq + 1) * 8, jh * 512:(jh + 1) * 512])])
                        d3 = sld[:, 0:KC * 512].rearrange("p (k c) -> p k c", c=512)
                        items = []
                        for jj in range(4):
                            for kc in range(KC):
                                items.append((banks[4 + jj][:, :], d3[:, kc, jj * 128:(jj + 1) * 128], gT[:, kq * 8 + kc, :],
                                              kq == 0 and kc == 0, kq == 3 and kc == KC - 1, {}))
                        P.op("pe", f_mms(items), reads=[t_sld, t_g[kq]], writes=[t_bank[4 + jj_] for jj_ in range(4)])
                    for jj in range(4):
                        P.op("act", f_acopy(ffT[:, jh * 4 + jj, :], banks[4 + jj][:, :]), reads=[t_bank[4 + jj]], writes=[t_ff])
                    if jh == 0 and tb + 1 < NB:
                        pre_norm_rest(tb + 1, 0)
                if l == 0 and tb == 0:
                    dbg_dump("ff0", ffT, t_ff, [128, KC, TB])
                post_norm_sq(tb)
                if tb + 1 < NB:
                    pending_post = tb
                else:
                    post_norm_rest(tb, 7)
            P.barrier(["pe", "act", "dve", "pool"] + (["sp"] if dbg else []))

        s_out = P.new_dma_sem()
        t_out = T("out")
        for c in range(KC):
            P.dma("sp", f_dma(outT[c * 128:(c + 1) * 128, :], hT[:, c, :]), s_out, reads=t_h, writes=[t_out])
        P.barrier(["sp", "pool", "act", "dve", "pe"])
        P.emit(nc, st)
    return nc, dbg_out


_CACHE = {}


def _get_prog(n_layers=2, dbg=()):
    key = (n_layers, tuple(dbg))
    if key not in _CACHE:
        _CACHE[key] = build(n_layers, dbg)
    return _CACHE[key]


def make_in_maps(inp):
    consts = make_consts()
    vecs = make_vecs(inp)
    shared = {
        "w_in": np.ascontiguousarray(inp["w_in"], dtype=np.float32),
        "w_a": np.ascontiguousarray(inp["w_branch_a"], dtype=np.float32),
        "w_b": np.ascontiguousarray(inp["w_branch_b"], dtype=np.float32),
        "w_c": np.ascontiguousarray(inp["w_branch_c"], dtype=np.float32),
        "w_o": np.ascontiguousarray(inp["w_out"], dtype=np.float32),
        "w_up": np.ascontiguousarray(inp["w_ffn_up"], dtype=np.float32),
        "w_dn": np.ascontiguousarray(inp["w_ffn_down"], dtype=np.float32),
        "vecs": vecs,
        "consts": consts,
    }
    maps = []
    for b in range(8):
        m = dict(shared)
        m["xT"] = np.ascontiguousarray(np.asarray(inp["x"][b], dtype=np.float32).T)
        maps.append(m)
    return maps


def kernel(**inputs):
    inp = {k: np.asarray(v) for k, v in inputs.items()}
    nc, _ = _get_prog(2, ())
    in_maps = make_in_maps(inp)
    res = run_bass_kernel_spmd(nc, in_maps, core_ids=list(range(8)))
    out = np.stack([np.ascontiguousarray(res.results[b]["outT"].T) for b in range(8)], axis=0)
    return out.astype(np.float32)
```

```python
import contextlib
import numpy as np
import concourse.bass as bass
import concourse.mybir as mybir
from concourse.bass_utils import run_bass_kernel_spmd
from concourse.alu_op_type import AluOpType as ALU

F32 = mybir.dt.float32
BF16 = mybir.dt.bfloat16
AF = mybir.ActivationFunctionType

S = 2048
D = 1024
NT = 16
NB = 4
TB = 512
KC = 8
DFF = 4096
EPS = 1e-6
NEG = -30000.0

ENGS = ["pe", "act", "dve", "pool", "sp"]
EPOCH = 30000


class T:
    __slots__ = ("name", "w", "r")

    def __init__(self, name):
        self.name = name
        self.w = None
        self.r = {}


class Prog:
    def __init__(self):
        self.streams = {e: [] for e in ENGS}
        self.count = {e: 0 for e in ENGS}
        self.seen = {e: {} for e in ENGS}
        self.dma_count = {}
        self.semkeys = set()
        self.n_dma_sems = 0

    def new_dma_sem(self):
        self.n_dma_sems += 1
        k = ("dma", self.n_dma_sems)
        self.dma_count[k] = 0
        self.semkeys.add(k)
        return k

    def _collect(self, eng, ident, reads, writes):
        waits = {}

        def need(dep, skip_same):
            if dep is None:
                return
            semkey, val, did = dep
            if skip_same and did == ident:
                return
            if self.seen[eng].get(semkey, 0) >= val:
                return
            if waits.get(semkey, 0) < val:
                waits[semkey] = val

        for t in reads:
            need(t.w, ident == "pe")
        for t in writes:
            need(t.w, True)
            for rid, (sk, v) in t.r.items():
                need((sk, v, rid), True)
        for sk, v in waits.items():
            self.seen[eng][sk] = v
        return list(waits.items())

    def op(self, eng, fn, reads=(), writes=()):
        waits = self._collect(eng, eng, reads, writes)
        c = self.count[eng]
        semkey = (eng, c // EPOCH)
        val = c % EPOCH + 1
        self.count[eng] = c + 1
        self.semkeys.add(semkey)
        self.streams[eng].append((waits, fn, semkey, 1))
        for t in reads:
            t.r[eng] = (semkey, val)
        for t in writes:
            t.w = (semkey, val, eng)
            t.r = {}

    def dma(self, q, fn, semkey, reads=(), writes=()):
        ident = semkey
        waits = self._collect(q, ident, reads, writes)
        self.dma_count[semkey] += 16
        val = self.dma_count[semkey]
        self.streams[q].append((waits, fn, semkey, 16))
        for t in reads:
            t.r[ident] = (semkey, val)
        for t in writes:
            t.w = (semkey, val, ident)
            t.r = {}

    @staticmethod
    def inherit(new_t, olds):
        for o in olds:
            if o.w is not None:
                new_t.r[("w", o.w[2], o.w[0])] = (o.w[0], o.w[1])
            for rid, (sk, v) in o.r.items():
                new_t.r[("r", rid, sk)] = (sk, v)

    def barrier(self, engs, dma=True):
        cur = []
        for f in ENGS:
            c = self.count[f]
            if c > 0:
                cur.append(((f, (c - 1) // EPOCH), (c - 1) % EPOCH + 1, f))
        if dma:
            for k, v in self.dma_count.items():
                if v > 0:
                    cur.append((k, v, k))
        for e in engs:
            waits = []
            for sk, v, f in cur:
                if f == e:
                    continue
                if self.seen[e].get(sk, 0) >= v:
                    continue
                self.seen[e][sk] = v
                waits.append((sk, v))
            if waits:
                self.streams[e].append((waits, None, None, 0))

    def emit(self, nc, stack):
        sems = {}
        for k in sorted(self.semkeys, key=str):
            sems[k] = stack.enter_context(nc.semaphore("s_%s_%s" % (k[0], k[1])))
        block = stack.enter_context(nc.Block())
        names = {"pe": "tensor", "act": "scalar", "dve": "vector", "pool": "gpsimd", "sp": "sync"}

        def run(engname):
            def body(e):
                for waits, fn, semkey, inc in self.streams[engname]:
                    for sk, v in waits:
                        e.wait_ge(sems[sk], v)
                    if fn is not None:
                        ins = fn(e)
                        ins.then_inc(sems[semkey], inc)
            return body

        for en in ENGS:
            if self.streams[en]:
                getattr(block, names[en])(run(en))


def f_mms(items):
    def fn(e):
        ins = None
        for (o, l, r, s, t, kw) in items:
            ins = e.matmul(o, lhsT=l, rhs=r, start=s, stop=t, **kw)
        return ins
    return fn


def f_act(out, in_, func, **kw):
    return lambda e: e.activation(out=out, in_=in_, func=func, **kw)


def f_acopy(out, in_):
    return lambda e: e.copy(out=out, in_=in_)


def f_tt(out, in0, in1, op):
    return lambda e: e.tensor_tensor(out=out, in0=in0, in1=in1, op=op)


def f_ts(out, in0, s1, s2, op0, op1):
    return lambda e: e.tensor_scalar(out=out, in0=in0, scalar1=s1, scalar2=s2, op0=op0, op1=op1)


def f_stt(out, in0, scalar, in1, op0, op1):
    return lambda e: e.scalar_tensor_tensor(out=out, in0=in0, scalar=scalar, in1=in1, op0=op0, op1=op1)


def f_copy(out, in_):
    return lambda e: e.tensor_copy(out=out, in_=in_)


def f_memset(ap, v):
    return lambda e: e.memset(ap, v)


def f_trs(items):
    def fn(e):
        ins = None
        for (o, i, idn) in items:
            ins = e.transpose(o, i, idn)
        return ins
    return fn


def f_dma(out, in_):
    return lambda e: e.dma_start(out=out, in_=in_)


def f_dmas(pairs):
    def fn(e):
        ins = None
        for (o, i) in pairs:
            ins = e.dma_start(out=o, in_=i)
        return ins
    return fn


C_ID = 0
C_MASK = 128
C_BLK = C_MASK + 8 * 128
C_VBLK = C_BLK + 128
C_RST = C_VBLK + 512
C_ONES = C_RST + 512
NCONST = C_ONES + 128

M_CAUSAL, M_OFFB, M_OFFA, M_4C, M_4, M_4U, M_16C, M_16 = range(8)

V_PER_L = 32 + 256 + 1 + 8
V_NPM, V_NPO, V_NPF, V_NPFF, V_CW, V_CB, V_OG, V_SK = 0, 8, 16, 24, 32, 224, 288, 289
V_LB = 2 * V_PER_L
NVEC = V_LB + 8


def make_consts():
    c = np.zeros((128, NCONST), np.float32)
    k = np.arange(128)[:, None]
    q = np.arange(128)[None, :]
    c[:, C_ID:C_ID + 128] = (k == q)
    valid = [None] * 8
    valid[M_CAUSAL] = (q >= k)
    valid[M_OFFB] = (q < k)
    valid[M_OFFA] = (q <= k)
    m4 = ((q - k) % 4 == 0)
    m16 = ((q - k) % 16 == 0)
    valid[M_4C] = m4 & (q >= k)
    valid[M_4] = m4
    valid[M_4U] = m4 & (q <= k)
    valid[M_16C] = m16 & (q >= k)
    valid[M_16] = m16
    for i in range(8):
        c[:, C_MASK + i * 128:C_MASK + (i + 1) * 128] = np.where(valid[i], 1.0, 0.0)
    c[:, C_BLK:C_BLK + 128] = ((k // 32) == (q // 32)) & (k <= q)
    vb = np.zeros((128, 4, 128), np.float32)
    for n in range(4):
        vb[32 * n:32 * n + 32, n, :] = 1.0
    c[:, C_VBLK:C_VBLK + 512] = vb.reshape(128, 512)
    rst = np.ones((128, 512), np.float32)
    rst[:, 0::32] = 0.0
    c[:, C_RST:C_RST + 512] = rst
    c[:, C_ONES:C_ONES + 128] = 1.0
    return c


def make_vecs(inp):
    v = np.zeros((128, NVEC), np.float32)
    pc = lambda a, n: np.ascontiguousarray(a.reshape(n, 128).T)
    for l in range(2):
        b = l * V_PER_L
        v[:, b + V_NPM:b + V_NPM + 8] = pc(inp["norm_pre_mix"][l], 8)
        v[:, b + V_NPO:b + V_NPO + 8] = pc(inp["norm_post_mix"][l], 8)
        v[:, b + V_NPF:b + V_NPF + 8] = pc(inp["norm_pre_ffn"][l], 8)
        v[:, b + V_NPFF:b + V_NPFF + 8] = pc(inp["norm_post_ffn"][l], 8)
        cw = inp["ffn_conv_w"][l]
        for j in range(3):
            v[:, b + V_CW + j * 64:b + V_CW + (j + 1) * 64] = pc(cw[j], 64)
        v[:, b + V_CB:b + V_CB + 64] = pc(inp["ffn_conv_b"][l], 64)
        v[:, b + V_OG] = inp["hgrn_out_norm"][l]
        v[:, b + V_SK:b + V_SK + 8] = inp["attn_sinks"][l][None, :]
        v[:, V_LB + l * 4:V_LB + l * 4 + 4] = pc(inp["hgrn_lb_logits"][l], 4)
    return v


SLOT_EL = 4352
ARENA_BYTES = 116 * 1024


def build(n_layers=2, dbg=()):
    nc = bass.Bass("TRN2", target_bir_lowering=False)
    dram = lambda n, s, dt=F32, kind="ExternalInput": nc.dram_tensor(n, s, dt, kind=kind).ap()
    xT = dram("xT", [D, S])
    w_in = dram("w_in", [2, D, 8192])
    w_a = dram("w_a", [2, 256, D])
    w_b = dram("w_b", [2, 512, D])
    w_c = dram("w_c", [2, 512, D])
    w_o = dram("w_o", [2, D, D])
    w_up = dram("w_up", [2, D, 8192])
    w_dn = dram("w_dn", [2, DFF, D])
    vecs_d = dram("vecs", [128, NVEC])
    cst_d = dram("consts", [128, NCONST])
    outT = dram("outT", [D, S], kind="ExternalOutput")
    dbg_out = {}

    st = contextlib.ExitStack()
    with st:
        sb = lambda n, s, d: st.enter_context(nc.sbuf_tensor(n, s, d))
        hT = sb("hT", [128, KC, S], F32)
        cst = sb("cst", [128, NCONST], BF16)
        vec = sb("vec", [128, NVEC], F32)
        small = sb("small", [128, 64], F32)
        slots = [sb("slot%d" % i, [128, SLOT_EL], BF16) for i in range(2)]
        arena = sb("arena", [128, ARENA_BYTES // 2], BF16)
        banks = [st.enter_context(nc.psum_tensor("pb%d" % i, [128, 512], F32)) for i in range(8)]

        P = Prog()
        t_bank = [T("pb%d" % i) for i in range(8)]
        t_h = [T("h%d" % i) for i in range(NB)]
        t_cst = T("cst")
        t_vec = T("vec")
        t_small = T("small")

        ident = cst[:, C_ID:C_ID + 128]
        ones_bf = cst[:, C_ONES:C_ONES + 128]
        mask_ap = lambda i: cst[:, C_MASK + i * 128:C_MASK + (i + 1) * 128]
        blk01 = cst[:, C_BLK:C_BLK + 128]
        vblk = cst[:, C_VBLK:C_VBLK + 512]
        rstm = cst[:, C_RST:C_RST + 512]
        eps_ap = small[:, 0:1]

        class Arena:
            def __init__(self):
                self.off = 0

            def alloc(self, nbytes):
                o = self.off
                self.off = (self.off + nbytes + 63) // 64 * 64
                assert self.off <= ARENA_BYTES, ("arena overflow", self.off)
                return o

            def f32(self, off, n):
                return arena[:, off // 2:off // 2 + 2 * n].bitcast(F32)

            def bf(self, off, n):
                return arena[:, off // 2:off // 2 + n]

        AR = Arena()

        class Ring:
            def __init__(self, aps):
                self.aps = list(aps)
                self.ts = [T("slot%d" % i) for i in range(len(aps))]
                self.sems = [P.new_dma_sem() for _ in aps]
                self.i = 0

            def load(self, pairs_fn):
                k = self.i % len(self.aps)
                self.i += 1
                ap = self.aps[k]
                for (o_, i_) in pairs_fn(ap):
                    if len(o_.shape) > 3:
                        for a_ in range(o_.shape[2]):
                            P.dma("pool", f_dma(o_[:, :, a_, :], i_[:, :, a_, :]), self.sems[k], writes=[self.ts[k]])
                    else:
                        P.dma("pool", f_dma(o_, i_), self.sems[k], writes=[self.ts[k]])
                return ap, self.ts[k]

        ring = Ring([s_[:] for s_ in slots])

        def dbg_dump(name, ap, t, shape):
            if name in dbg:
                d = nc.dram_tensor("dbg_" + name, shape, F32, kind="ExternalOutput").ap()
                dbg_out[name] = d
                P.dma("pool", f_dma(d, ap), P.new_dma_sem(), reads=[t], writes=[T("dbgo")])

        s_in = P.new_dma_sem()
        P.dma("sp", f_dma(vec[:], vecs_d), P.new_dma_sem(), writes=[t_vec])
        P.dma("pool", f_dma(cst[:], cst_d), P.new_dma_sem(), writes=[t_cst])
        for tb in range(NB):
            s_tb = P.new_dma_sem()
            for c in range(KC):
                P.dma("sp", f_dma(hT[:, c, tb * TB:(tb + 1) * TB], xT[c * 128:(c + 1) * 128, tb * TB:(tb + 1) * TB]), s_tb,
                      reads=([t_h[0]] if tb > 0 else []), writes=[t_h[tb]])
        P.op("dve", f_memset(small[:], 0.0), writes=[t_small])
        P.op("dve", f_memset(eps_ap, EPS), writes=[t_small])
        lg = vec[:, V_LB:V_LB + 8]
        tmpA = small[:, 48:56]
        tmpB = small[:, 56:60]
        P.op("act", f_act(tmpA, lg, AF.Exp), reads=[t_vec, t_small], writes=[t_small])
        P.op("dve", f_tt(tmpB, tmpA[:, 0:4], tmpA[:, 4:8], ALU.add), reads=[t_small], writes=[t_small])
        P.op("dve", lambda e: e.reciprocal(out=tmpB, in_=tmpB), reads=[t_small], writes=[t_small])
        P.op("dve", f_tt(tmpA[:, 0:4], tmpA[:, 0:4], tmpB, ALU.mult), reads=[t_small], writes=[t_small])
        P.op("dve", f_tt(tmpA[:, 4:8], tmpA[:, 4:8], tmpB, ALU.mult), reads=[t_small], writes=[t_small])
        P.op("dve", f_tt(tmpA[:, 4:8], tmpA[:, 4:8], tmpA[:, 0:4], ALU.add), reads=[t_small], writes=[t_small])
        P.op("dve", f_tt(small[:, 8:12], tmpA[:, 0:4], tmpA[:, 0:4], ALU.subtract), reads=[t_small], writes=[t_small])
        P.op("dve", f_tt(small[:, 12:16], tmpA[:, 4:8], tmpA[:, 0:4], ALU.subtract), reads=[t_small], writes=[t_small])
        P.op("dve", f_ts(small[:, 16:24], small[:, 8:16], -0.5, 0.5, ALU.mult, ALU.add), reads=[t_small], writes=[t_small])
        P.op("dve", f_ts(small[:, 24:32], small[:, 8:16], 0.5, -0.5, ALU.mult, ALU.add), reads=[t_small], writes=[t_small])
        P.op("dve", f_ts(small[:, 8:16], small[:, 8:16], 0.5, 0.5, ALU.mult, ALU.add), reads=[t_small], writes=[t_small])
        for l in range(2):
            sk = vec[:, l * V_PER_L + V_SK:l * V_PER_L + V_SK + 8]
            P.op("act", f_act(small[:, 32 + 8 * l:40 + 8 * l], sk, AF.Exp), reads=[t_vec, t_small], writes=[t_small])

        def rstd_block(src3, nch, ncols, reads, inv_n, sq_ap, t_sq, ln_ap, rstd_ap, t_r, bank):
            P.op("act", f_act(sq_ap, src3, AF.Square), reads=reads, writes=[t_sq])
            items = [(banks[bank][:, 0:ncols], ones_bf, sq_ap[:, c, :], c == 0, c == nch - 1, {}) for c in range(nch)]
            P.op("pe", f_mms(items), reads=[t_sq, t_cst], writes=[t_bank[bank]])
            P.op("act", f_act(ln_ap, banks[bank][:, 0:ncols], AF.Ln, bias=eps_ap, scale=inv_n),
                 reads=[t_bank[bank], t_small], writes=[t_r])
            P.op("act", f_act(rstd_ap, ln_ap, AF.Exp, scale=-0.5), reads=[t_r], writes=[t_r])

        def projT(bank, lhs_fn, rhsT, t_rhs, t_w, tb, ncols=TB, col0=None):
            c0 = tb * TB if col0 is None else col0
            items = [(banks[bank][:, 0:ncols], lhs_fn(kc), rhsT[:, kc, c0:c0 + ncols], kc == 0, kc == KC - 1, {})
                     for kc in range(KC)]
            P.op("pe", f_mms(items), reads=[t_w, t_rhs], writes=[t_bank[bank]])

        for l in range(n_layers):
            vb = l * V_PER_L
            if l > 0:
                P.barrier(["pe", "act", "dve"] + (["pool", "sp"] if dbg else []), dma=bool(dbg))
            o_hn, o_oc, o_oa, o_ob = 0, 32 * 1024, 48 * 1024, 56 * 1024
            AR.off = 72 * 1024
            hnT = AR.bf(o_hn, KC * S).rearrange("p (c t) -> p c t", t=S)
            t_hn = [T("hn%d" % i) for i in range(NB)]
            ocT = AR.bf(o_oc, 4 * S).rearrange("p (c t) -> p c t", t=S)
            oaT = AR.bf(o_oa, 2 * S).rearrange("p (c t) -> p c t", t=S)
            obT = AR.bf(o_ob, 4 * S).rearrange("p (c t) -> p c t", t=S)
            t_oc = [T("oc%d" % i) for i in range(NB)]
            t_oa = [T("oa%d" % i) for i in range(NB)]
            t_ob = [T("ob%d" % i) for i in range(NB)]
            base_mark = AR.off

            AR.off = 48 * 1024
            o_v = AR.alloc(NT * 512 * 2)
            v_tok = AR.bf(o_v, NT * 512).rearrange("p (t c) -> p t c", c=512)
            t_v = T("v_tok")
            NS = 8
            o_Sr = AR.alloc(NS * 128 * 4)
            Sring = AR.f32(o_Sr, NS * 128)
            Sst = [Sring[:, i * 128:(i + 1) * 128] for i in range(NS)]
            t_S = [T("S%d" % i) for i in range(NS)]
            o_Sb = [AR.alloc(512 * 2) for _ in range(2)]
            S_bf = [AR.bf(o, 512) for o in o_Sb]
            t_Sb = [T("Sbf%d" % i) for i in range(2)]
            o_sq1 = AR.alloc(TB * 2)
            sq1 = AR.bf(o_sq1, TB).rearrange("p (c t) -> p c t", c=1)
            t_sq1 = T("sq1")
            o_rs1 = AR.alloc(TB * 4)
            rs1 = AR.f32(o_rs1, TB)
            ln1 = rs1
            t_r1 = T("r1")
            NV = 2
            o_att = [AR.alloc(128 * 2) for _ in range(NV)]
            attm = [AR.bf(o, 128) for o in o_att]
            t_att = [T("attm%d" % i) for i in range(NV)]
            o_sq = o_oc
            sq = AR.bf(o_sq, KC * TB).rearrange("p (c t) -> p c t", t=TB)
            o_rs = o_oc + KC * TB * 2
            rs_pair = [AR.f32(o_rs, TB), AR.f32(o_rs + TB * 4, TB)]
            t_sq = T("sq")
            t_rp = [T("rstd0"), T("rstd1")]
            NX = 2
            Xo = [[AR.alloc(TB * 4) for _ in range(6)] for _ in range(NX)]
            X = [[AR.f32(o, TB) for o in xs] for xs in Xo]
            t_X = [[T("X%d_%d" % (s_, i)) for i in range(6)] for s_ in range(NX)]
            o_X4 = [AR.alloc(TB * 4) for _ in range(1)]
            X4s = [X[0][3], X[1][3], AR.f32(o_X4[0], TB)]
            t_X4s = [t_X[0][3], t_X[1][3], T("X4_2")]
            o_qe = [AR.alloc(TB * 2) for _ in range(NX)]
            QE = [AR.bf(o, TB) for o in o_qe]
            t_QE = [T("QE%d" % i) for i in range(NX)]
            o_ke = [AR.alloc(TB * 2) for _ in range(NX)]
            KE = [AR.bf(o, TB) for o in o_ke]
            t_KE = [T("KE%d" % i) for i in range(NX)]
            o_X7 = AR.alloc(TB * 4)
            X7 = AR.f32(o_X7, TB)
            t_X7 = T("X7")
            o_kd = [AR.alloc(TB * 2) for _ in range(2)]
            KD = [AR.bf(o, TB) for o in o_kd]
            t_KD = [T("KD%d" % i) for i in range(2)]
            o_kt = [AR.alloc(TB * 2) for _ in range(2)]
            kd_tok = [AR.bf(o, TB).rearrange("p (i c) -> p i c", c=128) for o in o_kt]
            t_kt = [T("kdtok%d" % i) for i in range(2)]
            o_vbd = [AR.alloc(512 * 2) for _ in range(NV)]
            vbd = [AR.bf(o, 512) for o in o_vbd]
            t_vbd = [T("vbd%d" % i) for i in range(NV)]

            slab, t_sl = ring.load(lambda ap: [(ap[:, 0:KC * 512].rearrange("p (k c) -> p k c", c=512),
                                               w_in[l].rearrange("(k p) c -> p k c", p=128)[:, :, 4096:4608])])
            slab3 = slab[:, 0:KC * 512].rearrange("p (k c) -> p k c", c=512)
            for tb in range(NB):
                tsl = slice(tb * TB, (tb + 1) * TB)
                rs_t, t_r = rs_pair[tb % 2], t_rp[tb % 2]
                rstd_block(hT[:, :, tsl], KC, TB, [t_h[tb]], 1.0 / D, sq, t_sq, rs_t, rs_t, t_r, 7)
                for c in range(KC):
                    P.op("dve", f_stt(hnT[:, c, tsl], hT[:, c, tsl], vec[:, vb + V_NPM + c:vb + V_NPM + c + 1],
                                      rs_t, ALU.mult, ALU.mult), reads=[t_h[tb], t_r, t_vec], writes=[t_hn[tb]])
                for tile in range(4 * tb, 4 * tb + 4):
                    bk_ = tile % 2
                    items = [(banks[bk_][:, :], hnT[:, kc, tile * 128:(tile + 1) * 128], slab3[:, kc, :], kc == 0, kc == KC - 1, {})
                             for kc in range(KC)]
                    P.op("pe", f_mms(items), reads=[t_sl, t_hn[tile // 4]], writes=[t_bank[bk_]])
                    P.op("act", f_acopy(v_tok[:, tile, :], banks[bk_][:, :]), reads=[t_bank[bk_]], writes=[t_v])
            if l == 0:
                dbg_dump("hn0", hnT, t_hn[3], [128, KC, S])
            for t_ in t_oc:
                P.inherit(t_, [t_sq] + t_rp)

            lb_c = lambda hd: small[:, 8 + 4 * l + hd:9 + 4 * l + hd]
            oml_c = lambda hd: small[:, 16 + 4 * l + hd:17 + 4 * l + hd]
            noml_c = lambda hd: small[:, 24 + 4 * l + hd:25 + 4 * l + hd]
            units = [(hd, blk) for hd in range(4) for blk in range(NB)]
            head_slab = {}
            cstate = {"sidx": 0}

            def S1_parts(u):
                hd, blk = units[u]
                xs = u % NX
                X1, X2, X3, X4, X5, X6 = X[xs]
                tX1, tX2, tX3, tX4, tX5, tX6 = t_X[xs]
                X4, tX4 = X4s[u % 3], t_X4s[u % 3]
                if blk == 0:
                    def pairs(ap, hd=hd):
                        wv = w_in[l].rearrange("(k p) c -> p k c", p=128)
                        a3 = ap[:, 0:KC * 384].rearrange("p (k a c) -> p k a c", a=3, c=128)
                        src2 = wv[:, :, 3072 + 128 * hd:3072 + 128 * hd + 1024].rearrange("p k (a b) -> p k a b", b=512)[:, :, :, 0:128]
                        return [(a3[:, :, 0:2, :], src2),
                                (a3[:, :, 2, :], wv[:, :, 4608 + 128 * hd:4608 + 128 * hd + 128])]
                    slab, t_sl_ = ring.load(pairs)
                    head_slab[hd] = (slab[:, 0:KC * 384].rearrange("p (k a c) -> p k a c", a=3, c=128), t_sl_)
                sl4, t_sl = head_slab[hd]

                def partA():
                    projT(0, lambda kc: sl4[:, kc, 0, :], hnT, t_hn[blk], t_sl, blk)
                    P.op("act", f_act(X1, banks[0][:, :], AF.Silu), reads=[t_bank[0]], writes=[tX1])
                    projT(1, lambda kc: sl4[:, kc, 1, :], hnT, t_hn[blk], t_sl, blk)
                    P.op("act", f_act(X2, banks[1][:, :], AF.Tanh, scale=0.5), reads=[t_bank[1]], writes=[tX2])

                def partB():
                    P.op("dve", f_ts(X3, X2, oml_c(hd), lb_c(hd), ALU.mult, ALU.add), reads=[tX2, t_small], writes=[tX3])
                    P.op("dve", f_ts(X2, X2, noml_c(hd), oml_c(hd), ALU.mult, ALU.add), reads=[tX2, t_small], writes=[tX2])
                    P.op("act", f_act(X3, X3, AF.Ln), reads=[tX3], writes=[tX3])

                def partC():
                    P.op("dve", lambda e, X4=X4, X3=X3: e.tensor_tensor_scan(out=X4, data0=rstm, data1=X3, initial=0.0,
                                                                             op0=ALU.mult, op1=ALU.add),
                         reads=[tX3, t_cst], writes=[tX4])
                    b3 = X4.rearrange("p (n c) -> p n c", c=32)
                    P.op("dve", f_tt(X3.rearrange("p (n c) -> p n c", c=32),
                                     X4[:, 31:TB:32].unsqueeze(2).to_broadcast([128, 16, 32]), b3, ALU.subtract),
                         reads=[tX4], writes=[tX3])
                    P.op("act", f_act(X5, X4, AF.Exp), reads=[tX4], writes=[tX5])
                    P.op("act", f_act(X6, X4, AF.Exp, scale=-1.0), reads=[tX4], writes=[tX6])
                    P.op("act", f_act(X3, X3, AF.Exp), reads=[tX3], writes=[tX3])

                def partD():
                    k2 = u % 2
                    P.op("dve", f_tt(KD[k2], X2, X3, ALU.mult), reads=[tX2, tX3], writes=[t_KD[k2]])
                    pbt = banks[3][:].bitcast(BF16)
                    P.op("pe", f_trs([(pbt[:, i * 128:(i + 1) * 128], KD[k2][:, i * 128:(i + 1) * 128], ident) for i in range(4)]),
                         reads=[t_KD[k2], t_cst], writes=[t_bank[3]])
                    P.op("act", f_acopy(kd_tok[k2], pbt[:, 0:512].rearrange("p (i c) -> p i c", c=128)),
                         reads=[t_bank[3]], writes=[t_kt[k2]])
                    P.op("dve", f_tt(QE[xs], X1, X5, ALU.mult), reads=[tX1, tX5], writes=[t_QE[xs]])
                    P.op("dve", f_tt(KE[xs], X2, X6, ALU.mult), reads=[tX2, tX6], writes=[t_KE[xs]])
                return [partA, partB, partC, partD]

            F32R = mybir.dt.float32r
            USE_R = False
            r32 = (lambda ap: ap.bitcast(F32R)) if USE_R else (lambda ap: ap)
            tctr = {"n": 0}

            def unit_ctx(g):
                u, i = divmod(g, 4)
                hd, blk = units[u]
                xs = u % NX
                tile = blk * 4 + i
                return u, i, hd, blk, xs, tile, tile % NV, (4 if g % 2 == 0 else 2)

            def st_V(g):
                u, i, hd, blk, xs, tile, vi, ub = unit_ctx(g)
                vh = v_tok[:, tile, hd * 128:(hd + 1) * 128]
                P.op("dve", f_tt(vbd[vi].rearrange("p (n c) -> p n c", c=128),
                                 vh.unsqueeze(1).to_broadcast([128, 4, 128]),
                                 vblk.rearrange("p (n c) -> p n c", c=128), ALU.mult),
                     reads=[t_v, t_cst], writes=[t_vbd[vi]])

            def st_UA(g):
                u, i, hd, blk, xs, tile, vi, ub = unit_ctx(g)
                P.op("pe", f_mms([(banks[ub][:, :], kd_tok[u % 2][:, i, :], vbd[vi], True, True, {})]),
                     reads=[t_kt[u % 2], t_vbd[vi]], writes=[t_bank[ub]])
                P.op("pe", f_mms([(banks[5][:, 0:128], KE[xs][:, i * 128:(i + 1) * 128], QE[xs][:, i * 128:(i + 1) * 128], True, True, {})]),
                     reads=[t_KE[xs], t_QE[xs]], writes=[t_bank[5]])

            def st_M(g):
                u, i, hd, blk, xs, tile, vi, ub = unit_ctx(g)
                P.op("dve", f_tt(attm[vi], banks[5][:, 0:128], blk01, ALU.mult),
                     reads=[t_bank[5], t_cst], writes=[t_att[vi]])

            def st_S(g):
                u, i, hd, blk, xs, tile, vi, ub = unit_ctx(g)
                X5, tX5 = X[xs][4], t_X[xs][4]
                sidx = 4 * g
                if blk == 0 and i == 0:
                    P.op("dve", f_memset(Sst[sidx % NS], 0.0), writes=[t_S[sidx % NS]])
                half = g % 2
                for n_ in range(4):
                    cs = i * 128 + n_ * 32
                    s_cur = (sidx + n_) % NS
                    s_nxt = (sidx + n_ + 1) % NS
                    if n_ == 3:
                        P.op("act", f_acopy(S_bf[half], Sring[:, half * 512:(half + 1) * 512]),
                             reads=[t_S[(sidx + k_) % NS] for k_ in range(4)], writes=[t_Sb[half]])
                    P.op("dve", f_stt(Sst[s_nxt], Sst[s_cur], X5[:, cs + 31:cs + 32], banks[ub][:, n_ * 128:(n_ + 1) * 128],
                                      ALU.mult, ALU.add),
                         reads=[t_S[s_cur], tX5, t_bank[ub]], writes=[t_S[s_nxt]])

            def st_O(g):
                u, i, hd, blk, xs, tile, vi, ub = unit_ctx(g)
                half = g % 2
                vh = v_tok[:, tile, hd * 128:(hd + 1) * 128]
                items = [(banks[6][:, i * 128:(i + 1) * 128], vh, attm[vi], i == 0, False, {"skip_group_check": True})]
                rds = [t_v, t_att[vi], t_QE[xs], t_Sb[half]]
                for n_ in range(4):
                    cs = i * 128 + n_ * 32
                    items.append((banks[6][:, cs:cs + 32], S_bf[half][:, n_ * 128:(n_ + 1) * 128], QE[xs][:, cs:cs + 32], False,
                                  (i == 3 and n_ == 3), {"skip_group_check": True}))
                P.op("pe", f_mms(items), reads=rds, writes=[t_bank[6]])

            def S2_end1(u):
                X4, tX4 = X4s[u % 3], t_X4s[u % 3]
                P.op("act", f_acopy(X4, banks[6][:, :]), reads=[t_bank[6]], writes=[tX4])
                P.op("act", f_act(sq1, X4.rearrange("p (c t) -> p c t", c=1), AF.Square), reads=[tX4], writes=[t_sq1])

            def S2_end2(u):
                hd, blk = units[u]
                X4, tX4 = X4s[u % 3], t_X4s[u % 3]
                tsl = slice(blk * TB, (blk + 1) * TB)
                P.op("pe", f_mms([(banks[7][:, :], ones_bf, sq1[:, 0, :], True, True, {})]), reads=[t_sq1, t_cst], writes=[t_bank[7]])
                P.op("act", f_act(rs1, banks[7][:, :], AF.Ln, bias=eps_ap, scale=1.0 / 128), reads=[t_bank[7], t_small], writes=[t_r1])
                P.op("act", f_act(rs1, rs1, AF.Exp, scale=-0.5), reads=[t_r1], writes=[t_r1])
                P.op("dve", f_tt(X4, X4, rs1, ALU.mult), reads=[tX4, t_r1], writes=[tX4])
                sl4g, t_slg = head_slab[hd]
                projT(1, lambda kc: sl4g[:, kc, 2, :], hnT, t_hn[blk], t_slg, blk)
                P.op("act", f_act(X7, banks[1][:, :], AF.Silu), reads=[t_bank[1]], writes=[t_X7])
                P.op("dve", f_stt(ocT[:, hd, tsl], X4, vec[:, vb + V_OG:vb + V_OG + 1], X7, ALU.mult, ALU.mult),
                     reads=[tX4, t_X7, t_vec], writes=[t_oc[blk]])

            NU_ = len(units)
            parts = {}

            def part(u, k):
                if u >= NU_:
                    return
                if u not in parts:
                    parts[u] = S1_parts(u)
                parts[u][k]()

            for k in range(4):
                part(0, k)
            part(1, 0)
            part(1, 1)
            pend = None
            NG = 4 * NU_
            st_V(0)
            st_UA(0)
            st_M(0)
            st_V(1)
            for g in range(NG):
                u, i = divmod(g, 4)
                if i == 3:
                    part(u + 1, 3)
                st_S(g)
                if g + 1 < NG:
                    st_UA(g + 1)
                st_O(g)
                if g + 2 < NG:
                    st_V(g + 2)
                if g + 1 < NG:
                    st_M(g + 1)
                if i == 0:
                    if pend is not None:
                        S2_end2(pend)
                        pend = None
                    part(u + 2, 0)
                elif i == 1:
                    part(u + 1, 2)
                elif i == 2:
                    part(u + 2, 1)
                else:
                    S2_end1(u)
                    pend = u
            S2_end2(pend)
            if l == 0:
                dbg_dump("oc0", ocT, t_oc[3], [128, 4, S])

            P.barrier(["pe", "act", "dve"] + (["pool", "sp"] if dbg else []), dma=bool(dbg))
            AR.off = 56 * 1024
            o_aq = AR.alloc(2 * S * 2)
            o_ak = AR.alloc(4 * S * 2)
            aqT = AR.bf(o_aq, 2 * S).rearrange("p (c t) -> p c t", t=S)
            akz = AR.bf(o_ak, 4 * S).rearrange("p (c h t) -> p c h t", h=2, t=S)
            P.op("dve", f_memset(AR.bf(o_ak, 4 * S), 0.0), writes=[t_ak_all := T("akz")])
            t_aq = [T("aq%d" % i) for i in range(NB)]
            t_ak = [T("ak%d" % i) for i in range(NB)]
            o_av = AR.alloc(NT * 4 * 65 * 2)
            av = AR.bf(o_av, NT * 260).rearrange("p (t h d) -> p t h d", h=4, d=65)
            t_av = T("av")
            o_oacc = AR.alloc(NT * 260 * 4)
            oacc = AR.f32(o_oacc, NT * 260).rearrange("p (t c) -> p t c", c=260)
            t_oacc = [T("oacc%d" % i) for i in range(NT)]
            NPT = 6
            o_pt = [AR.alloc(512 * 2) for _ in range(NPT)]
            PT = [AR.bf(o, 512) for o in o_pt]
            t_pt = [T("pt%d" % i) for i in range(NPT)]
            o_otk = [AR.alloc(512 * 2) for _ in range(2)]
            otk = [AR.bf(o, 512) for o in o_otk]
            t_otk = [T("otk%d" % i) for i in range(2)]
            o_rd = AR.alloc(64)
            rden = AR.f32(o_rd, 8)
            t_rd = T("rden")

            class AttnPipe:
                def __init__(self, PT, t_pt, sbanks):
                    self.PT, self.t_pt = PT, t_pt
                    self.sbanks = sbanks
                    self.LA = min(len(sbanks), len(PT)) - 1
                    self.q = []
                    self.rot = 0

                def add(self, chunk, obank, first, post):
                    sbk = self.sbanks[self.rot % len(self.sbanks)]
                    pti = self.rot % len(self.PT)
                    self.rot += 1
                    PTb, tptb = self.PT[pti], self.t_pt[pti]
                    n = len(chunk) * 128
                    mi0 = chunk[0][2]
                    assert all(c_[2] == mi0 for c_ in chunk)
                    items = []
                    rds = []
                    for s_, (kT, qT, mi, vap, ocol, r_) in enumerate(chunk):
                        items.append((banks[sbk][:, s_ * 128:(s_ + 1) * 128], kT, qT, True, True, {}))
                        rds += r_
                    P.op("pe", f_mms(items), reads=rds, writes=[t_bank[sbk]])
                    P.op("act", f_act(PTb[:, 0:n], banks[sbk][:, 0:n], AF.Exp, scale=0.125),
                         reads=[t_bank[sbk]], writes=[tptb])
                    P.op("dve", f_tt(PTb[:, 0:n].rearrange("p (a c) -> p a c", c=128),
                                      PTb[:, 0:n].rearrange("p (a c) -> p a c", c=128),
                                      mask_ap(mi0).unsqueeze(1).to_broadcast([128, len(chunk), 128]), ALU.mult),
                         reads=[tptb, t_cst], writes=[tptb])
                    self.q.append((chunk, obank, first, post, PTb, tptb))
                    while len(self.q) > self.LA:
                        self._pv()

                def _pv(self):
                    chunk, obank, first, post, PTb, tptb = self.q.pop(0)
                    items = []
                    rds = [tptb]
                    for s_, (kT, qT, mi, vap, ocol, r_) in enumerate(chunk):
                        items.append((banks[obank][:, ocol:ocol + 65], PTb[:, s_ * 128:(s_ + 1) * 128], vap, first and s_ == 0, False,
                                      {"skip_group_check": True}))
                        rds += r_
                    P.op("pe", f_mms(items), reads=rds, writes=[t_bank[obank]])
                    if post is not None:
                        post()

                def flush(self):
                    while self.q:
                        self._pv()

            def attention(pipe, combos, obank, post):
                nb_ = (len(combos) + 3) // 4
                for bi in range(nb_):
                    pipe.add(combos[bi * 4:bi * 4 + 4], obank, bi == 0, post if bi == nb_ - 1 else None)

            pipe = AttnPipe(PT, t_pt, [2, 3, 4, 0, 1, 7])
            for g in range(3):
                def pairs(ap, g=g):
                    wv = w_in[l].rearrange("(k p) c -> p k c", p=128)
                    src = wv[:, :, 0:1536].rearrange("p k (a b) -> p k a b", b=768)[:, :, :, 256 * g:256 * g + 256]
                    return [(ap[:, 0:KC * 512].rearrange("p (k a c) -> p k a c", a=2, c=256), src)]
                slab, t_sl = ring.load(pairs)
                sl4 = slab[:, 0:KC * 512].rearrange("p (k a c) -> p k a c", a=2, c=256)
                slab_v, t_slv = ring.load(lambda ap, g=g: [(ap[:, 0:KC * 256].rearrange("p (k c) -> p k c", c=256),
                                                            w_in[l].rearrange("(k p) c -> p k c", p=128)[:, :, 1536 + 256 * g:1536 + 256 * g + 256])])
                slv3 = slab_v[:, 0:KC * 256].rearrange("p (k c) -> p k c", c=256)
                for tb in range(NB):
                    tsl = slice(tb * TB, (tb + 1) * TB)
                    for j in range(2):
                        bk_ = (2 * tb + j) % 2
                        projT(bk_, lambda kc, j=j: sl4[:, kc, 0, j * 128:(j + 1) * 128], hnT, t_hn[tb], t_sl, tb)
                        P.op("act", f_acopy(aqT[:, j, tsl], banks[bk_][:, :]), reads=[t_bank[bk_]], writes=[t_aq[tb]])
                    for j in range(2):
                        bk_ = (2 * tb + j) % 2
                        projT(bk_, lambda kc, j=j: sl4[:, kc, 1, j * 128:(j + 1) * 128], hnT, t_hn[tb], t_sl, tb)
                        P.op("act", f_acopy(akz[0:64, j, 0, tsl], banks[bk_][0:64, :]), reads=[t_bank[bk_], t_ak_all], writes=[t_ak[tb]])
                        P.op("act", f_acopy(akz[64:128, j, 1, tsl], banks[bk_][64:128, :]), reads=[t_bank[bk_], t_ak_all], writes=[t_ak[tb]])
                P.op("dve", f_memset(av.rearrange("p t h d -> p (t h) d")[:, :, 64:65], 1.0), writes=[t_av])
                for tile in range(NT):
                    bk_ = tile % 2
                    items = [(banks[bk_][:, 0:256], hnT[:, kc, tile * 128:(tile + 1) * 128], slv3[:, kc, :], kc == 0, kc == KC - 1, {})
                             for kc in range(KC)]
                    P.op("pe", f_mms(items), reads=[t_slv, t_hn[tile // 4]], writes=[t_bank[bk_]])
                    P.op("act", f_acopy(av[:, tile, :, 0:64], banks[bk_][:, 0:256].rearrange("p (h d) -> p h d", d=64)),
                         reads=[t_bank[bk_]], writes=[t_av])
                for T_ in range(NT):
                    if g == 0:
                        Js = [(T_, M_CAUSAL)] + ([(T_ - 1, M_OFFA)] if T_ >= 1 else [])
                    elif g == 1:
                        Js = [(T_, M_4C)] + [(T_ - d_, M_4) for d_ in (1, 2, 3) if T_ - d_ >= 0] + ([(T_ - 4, M_4U)] if T_ >= 4 else [])
                    else:
                        Js = [(T_, M_16C)] + [(J, M_16) for J in range(T_)]
                    combos = []
                    for (J, mi) in Js:
                        for hh in range(4):
                            j, half = hh // 2, hh % 2
                            combos.append((akz[:, j, half, J * 128:(J + 1) * 128], aqT[:, j, T_ * 128:(T_ + 1) * 128], mi,
                                           av[:, J, hh, :], hh * 65, [t_ak[J // 4], t_aq[T_ // 4], t_av]))
                    ob_ = 5 + (T_ % 2)

                    def post(T_=T_, ob_=ob_, g=g):
                        if g == 0:
                            P.op("dve", f_copy(oacc[:, T_, :], banks[ob_][:, 0:260]), reads=[t_bank[ob_]], writes=[t_oacc[T_]])
                        else:
                            P.op("dve", f_tt(oacc[:, T_, :], banks[ob_][:, 0:260], oacc[:, T_, :], ALU.add),
                                 reads=[t_bank[ob_], t_oacc[T_]], writes=[t_oacc[T_]])
                    attention(pipe, combos, ob_, post)
                pipe.flush()
            for T_ in range(NT):
                o3 = oacc[:, T_, :].rearrange("p (h d) -> p h d", d=65)
                oi = T_ % 2
                P.op("dve", lambda e, o3=o3, rd=rden: e.reciprocal(out=rd[:, 0:4], in_=o3[:, :, 64]), reads=[t_oacc[T_]], writes=[t_rd])
                P.op("dve", f_tt(otk[oi][:, 0:256].rearrange("p (h d) -> p h d", d=64), o3[:, :, 0:64],
                                 rden[:, 0:4].unsqueeze(2).to_broadcast([128, 4, 64]), ALU.mult),
                     reads=[t_oacc[T_], t_rd], writes=[t_otk[oi]])
                pbt = banks[7][:].bitcast(BF16)
                P.op("pe", f_trs([(pbt[:, i * 128:(i + 1) * 128], otk[oi][:, i * 128:(i + 1) * 128], ident) for i in range(2)]),
                     reads=[t_otk[oi], t_cst], writes=[t_bank[7]])
                P.op("act", f_acopy(oaT[:, :, T_ * 128:(T_ + 1) * 128], pbt[:, 0:256].rearrange("p (i c) -> p i c", c=128)),
                     reads=[t_bank[7]], writes=[t_oa[T_ // 4]])
            if l == 0:
                dbg_dump("oa0", oaT, t_oa[3], [128, 2, S])

            P.barrier(["pe", "act", "dve"] + (["pool", "sp"] if dbg else []), dma=bool(dbg))
            AR.off = base_mark
            o_bq = AR.alloc(4 * S * 2)
            bqT = AR.bf(o_bq, 4 * S).rearrange("p (c t) -> p c t", t=S)
            o_bk = AR.alloc(4 * S * 2)
            bkz = AR.bf(o_bk, 4 * S).rearrange("p (k h t) -> p k h t", h=2, t=S)
            P.op("dve", f_memset(AR.bf(o_bk, 4 * S), 0.0), writes=[t_bk_all := T("bkz")])
            t_bq = [T("bq%d" % i) for i in range(NB)]
            t_bk = [T("bk%d" % i) for i in range(NB)]
            o_bv = AR.alloc(NT * 2 * 65 * 2)
            bv = AR.bf(o_bv, NT * 130).rearrange("p (t h d) -> p t h d", h=2, d=65)
            t_bv = T("bv")
            o_pt = [AR.alloc(512 * 2) for _ in range(5)]
            PT = [AR.bf(o, 512) for o in o_pt]
            t_pt = [T("ptb%d" % i) for i in range(5)]
            o_otk = [AR.alloc(512 * 2) for _ in range(2)]
            otk = [AR.bf(o, 512) for o in o_otk]
            t_otk = [T("otkb%d" % i) for i in range(2)]
            o_rd = AR.alloc(64)
            rden = AR.f32(o_rd, 8)
            t_rd = T("rdenb")

            slab_q, t_slq = ring.load(lambda ap: [(ap[:, 0:KC * 512].rearrange("p (k c) -> p k c", c=512),
                                                   w_in[l].rearrange("(k p) c -> p k c", p=128)[:, :, 2304:2816])])
            sq3 = slab_q[:, 0:KC * 512].rearrange("p (k c) -> p k c", c=512)

            def pairs(ap):
                wv = w_in[l].rearrange("(k p) c -> p k c", p=128)
                a3 = ap[:, 0:KC * 384].rearrange("p (k c) -> p k c", c=384)
                return [(a3[:, :, 0:256], wv[:, :, 2816:3072]),
                        (a3[:, :, 256:320], wv[:, :, 2880:2944]),
                        (a3[:, :, 320:384], wv[:, :, 2816:2880])]
            slab_k, t_slk = ring.load(pairs)
            sk3 = slab_k[:, 0:KC * 384].rearrange("p (k c) -> p k c", c=384)
            for tb in range(NB):
                tsl = slice(tb * TB, (tb + 1) * TB)
                for j in range(4):
                    bk_ = j % 2
                    projT(bk_, lambda kc, j=j: sq3[:, kc, j * 128:(j + 1) * 128], hnT, t_hn[tb], t_slq, tb)
                    P.op("act", f_acopy(bqT[:, j, tsl], banks[bk_][:, :]), reads=[t_bank[bk_]], writes=[t_bq[tb]])
                for v_ in range(2):
                    bk_ = v_ % 2
                    c0 = 0 if v_ == 0 else 256
                    projT(bk_, lambda kc, c0=c0: sk3[:, kc, c0:c0 + 128], hnT, t_hn[tb], t_slk, tb)
                    kv_lo, kv_hi = (0, 1) if v_ == 0 else (1, 0)
                    P.op("act", f_acopy(bkz[0:64, kv_lo, 0, tsl], banks[bk_][0:64, :]), reads=[t_bank[bk_], t_bk_all], writes=[t_bk[tb]])
                    P.op("act", f_acopy(bkz[64:128, kv_hi, 1, tsl], banks[bk_][64:128, :]), reads=[t_bank[bk_], t_bk_all], writes=[t_bk[tb]])
            P.op("dve", f_memset(bv.rearrange("p t h d -> p (t h) d")[:, :, 64:65], 1.0), writes=[t_bv])
            for tile in range(NT):
                bk_ = tile % 2
                items = [(banks[bk_][:, 0:128], hnT[:, kc, tile * 128:(tile + 1) * 128], sk3[:, kc, 128:256], kc == 0, kc == KC - 1, {})
                         for kc in range(KC)]
                P.op("pe", f_mms(items), reads=[t_slk, t_hn[tile // 4]], writes=[t_bank[bk_]])
                P.op("act", f_acopy(bv[:, tile, :, 0:64], banks[bk_][:, 0:128].rearrange("p (h d) -> p h d", d=64)),
                     reads=[t_bank[bk_]], writes=[t_bv])
            esk = small[:, 32 + 8 * l:40 + 8 * l]
            pipe = AttnPipe(PT, t_pt, [2, 3, 4, 0, 1])
            for T_ in range(NT):
                Js = [(T_, M_CAUSAL)] + ([(T_ - 1, M_OFFB)] if T_ >= 1 else [])
                oi = T_ % 2
                for hq in range(2):
                    combos = []
                    for (J, mi) in Js:
                        for h4 in range(4):
                            h = hq * 4 + h4
                            j, half, kv = h // 2, h % 2, h // 4
                            combos.append((bkz[:, kv, half, J * 128:(J + 1) * 128], bqT[:, j, T_ * 128:(T_ + 1) * 128], mi,
                                           bv[:, J, kv, :], h4 * 65, [t_bk[J // 4], t_bq[T_ // 4], t_bv]))
                    ob_ = 5 + hq

                    def post(T_=T_, ob_=ob_, hq=hq, oi=oi, rden=rden, otk=otk, t_otk=t_otk, t_rd=t_rd):
                        o3 = banks[ob_][:, 0:260].rearrange("p (h d) -> p h d", d=65)
                        P.op("dve", f_tt(rden[:, 4 * hq:4 * hq + 4], o3[:, :, 64], esk[:, hq * 4:hq * 4 + 4], ALU.add),
                             reads=[t_bank[ob_], t_small], writes=[t_rd])
                        P.op("dve", lambda e, rd=rden[:, 4 * hq:4 * hq + 4]: e.reciprocal(out=rd, in_=rd), reads=[t_rd], writes=[t_rd])
                        P.op("dve", f_tt(otk[oi][:, hq * 256:(hq + 1) * 256].rearrange("p (h d) -> p h d", d=64), o3[:, :, 0:64],
                                         rden[:, 4 * hq:4 * hq + 4].unsqueeze(2).to_broadcast([128, 4, 64]), ALU.mult),
                             reads=[t_bank[ob_], t_rd], writes=[t_otk[oi]])
                        if hq == 1:
                            pbt = banks[7][:].bitcast(BF16)
                            P.op("pe", f_trs([(pbt[:, i * 128:(i + 1) * 128], otk[oi][:, i * 128:(i + 1) * 128], ident) for i in range(4)]),
                                 reads=[t_otk[oi], t_cst], writes=[t_bank[7]])
                            P.op("act", f_acopy(obT[:, :, T_ * 128:(T_ + 1) * 128], pbt[:, 0:512].rearrange("p (i c) -> p i c", c=128)),
                                 reads=[t_bank[7]], writes=[t_ob[T_ // 4]])
                    attention(pipe, combos, ob_, post)
            pipe.flush()
            if l == 0:
                dbg_dump("ob0", obT, t_ob[3], [128, 4, S])

            P.barrier(["pe", "act", "dve"] + (["pool", "sp"] if dbg else []), dma=bool(dbg))
            AR.off = base_mark
            o_mx = AR.alloc(KC * S * 2)
            mixT = AR.bf(o_mx, KC * S).rearrange("p (c t) -> p c t", t=S)
            t_mx = [T("mixT%d" % i) for i in range(NB)]
            o_sg = [AR.alloc(TB * 4) for _ in range(3)]
            sg = [AR.f32(o, TB) for o in o_sg]
            t_sg = [T("sg%d" % i) for i in range(3)]
            for j in range(KC):
                def pairs(ap, j=j):
                    wv = w_in[l].rearrange("(k p) c -> p k c", p=128)
                    g4 = ap[:, 0:KC * 384].rearrange("p (k a c) -> p k a c", a=3, c=128)
                    src = wv[:, :, 5120:8192].rearrange("p k (a b) -> p k a b", b=1024)[:, :, :, j * 128:(j + 1) * 128]
                    o0 = KC * 384
                    wa3 = ap[:, o0:o0 + 256].rearrange("p (k c) -> p k c", c=128)
                    wb3 = ap[:, o0 + 256:o0 + 768].rearrange("p (k c) -> p k c", c=128)
                    wc3 = ap[:, o0 + 768:o0 + 1280].rearrange("p (k c) -> p k c", c=128)
                    return [(g4, src),
                            (wa3, w_a[l].rearrange("(k p) c -> p k c", p=128)[:, :, j * 128:(j + 1) * 128]),
                            (wb3, w_b[l].rearrange("(k p) c -> p k c", p=128)[:, :, j * 128:(j + 1) * 128]),
                            (wc3, w_c[l].rearrange("(k p) c -> p k c", p=128)[:, :, j * 128:(j + 1) * 128])]
                slab, t_sl = ring.load(pairs)
                g4 = slab[:, 0:KC * 384].rearrange("p (k a c) -> p k a c", a=3, c=128)
                o0 = KC * 384
                wa3 = slab[:, o0:o0 + 256].rearrange("p (k c) -> p k c", c=128)
                wb3 = slab[:, o0 + 256:o0 + 768].rearrange("p (k c) -> p k c", c=128)
                wc3 = slab[:, o0 + 768:o0 + 1280].rearrange("p (k c) -> p k c", c=128)
                for tb in range(NB):
                    tsl = slice(tb * TB, (tb + 1) * TB)
                    for i in range(3):
                        projT(i, lambda kc, i=i: g4[:, kc, i, :], hnT, t_hn[tb], t_sl, tb)
                        P.op("act", f_act(sg[i], banks[i][:, :], AF.Sigmoid), reads=[t_bank[i]], writes=[t_sg[i]])
                    for i, (w3, nk, oT_, t_o) in enumerate([(wa3, 2, oaT, t_oa), (wb3, 4, obT, t_ob), (wc3, 4, ocT, t_oc)]):
                        items = [(banks[3 + i][:, :], w3[:, kc, :], oT_[:, kc, tsl], kc == 0, kc == nk - 1, {}) for kc in range(nk)]
                        P.op("pe", f_mms(items), reads=[t_sl, t_o[tb]], writes=[t_bank[3 + i]])
                    P.op("dve", f_tt(sg[0], sg[0], banks[3][:, :], ALU.mult), reads=[t_sg[0], t_bank[3]], writes=[t_sg[0]])
                    P.op("dve", f_tt(sg[1], sg[1], banks[4][:, :], ALU.mult), reads=[t_sg[1], t_bank[4]], writes=[t_sg[1]])
                    P.op("dve", f_tt(sg[0], sg[0], sg[1], ALU.add), reads=[t_sg[0], t_sg[1]], writes=[t_sg[0]])
                    P.op("dve", f_tt(sg[2], sg[2], banks[5][:, :], ALU.mult), reads=[t_sg[2], t_bank[5]], writes=[t_sg[2]])
                    P.op("dve", f_tt(mixT[:, j, tsl], sg[0], sg[2], ALU.add), reads=[t_sg[0], t_sg[2]], writes=[t_mx[tb]])

            P.barrier(["pe", "act", "dve"] + (["pool", "sp"] if dbg else []), dma=bool(dbg))
            AR.off = 0
            o_mo = [AR.alloc(KC * TB * 4) for _ in range(2)]
            moT2 = [AR.f32(o, KC * TB).rearrange("p (c t) -> p c t", t=TB) for o in o_mo]
            t_mo2 = [T("moT%d" % i) for i in range(2)]
            o_sq = [AR.alloc(KC * TB * 2) for _ in range(2)]
            sq2 = [AR.bf(o, KC * TB).rearrange("p (c t) -> p c t", t=TB) for o in o_sq]
            t_sq2 = [T("sqg%d" % i) for i in range(2)]
            o_rs = AR.alloc(TB * 4)
            rs_t = AR.f32(o_rs, TB)
            t_r = T("rstdg")
            assert AR.off <= base_mark
            wo_slabs = []
            for jh in range(2):
                slab, t_sl = ring.load(lambda ap, jh=jh: [(ap[:, 0:KC * 512].rearrange("p (k c) -> p k c", c=512),
                                                           w_o[l].rearrange("(k p) c -> p k c", p=128)[:, :, jh * 512:(jh + 1) * 512])])
                wo_slabs.append((slab[:, 0:KC * 512].rearrange("p (k c) -> p k c", c=512), t_sl))

            def g2_rest(tb):
                tsl = slice(tb * TB, (tb + 1) * TB)
                mo, tmo = moT2[tb % 2], t_mo2[tb % 2]
                sq_, tsq_ = sq2[tb % 2], t_sq2[tb % 2]
                items = [(banks[5][:, :], ones_bf, sq_[:, c, :], c == 0, c == KC - 1, {}) for c in range(KC)]
                P.op("pe", f_mms(items), reads=[tsq_, t_cst], writes=[t_bank[5]])
                P.op("act", f_act(rs_t, banks[5][:, :], AF.Ln, bias=eps_ap, scale=1.0 / D), reads=[t_bank[5], t_small], writes=[t_r])
                P.op("act", f_act(rs_t, rs_t, AF.Exp, scale=-0.5), reads=[t_r], writes=[t_r])
                P.op("dve", f_tt(mo, mo, rs_t.unsqueeze(1).to_broadcast([128, KC, TB]), ALU.mult), reads=[tmo, t_r], writes=[tmo])
                for c in range(KC):
                    P.op("dve", f_stt(hT[:, c, tsl], mo[:, c, :], vec[:, vb + V_NPO + c:vb + V_NPO + c + 1], hT[:, c, tsl],
                                      ALU.mult, ALU.add), reads=[tmo, t_vec, t_h[tb]], writes=[t_h[tb]])

            pend = None
            for tb in range(NB):
                tsl = slice(tb * TB, (tb + 1) * TB)
                mo, tmo = moT2[tb % 2], t_mo2[tb % 2]
                for j in range(KC):
                    s3, t_sl = wo_slabs[j // 4]
                    jj = j % 4
                    bk_ = 6 + (j % 2)
                    items = [(banks[bk_][:, :], s3[:, kc, jj * 128:(jj + 1) * 128], mixT[:, kc, tsl], kc == 0, kc == KC - 1, {})
                             for kc in range(KC)]
                    P.op("pe", f_mms(items), reads=[t_sl, t_mx[tb]], writes=[t_bank[bk_]])
                    P.op("act", f_acopy(mo[:, j, :], banks[bk_][:, :]), reads=[t_bank[bk_]], writes=[tmo])
                    if j == 3 and pend is not None:
                        g2_rest(pend)
                        pend = None
                if l == 0 and tb == 0:
                    dbg_dump("mixo0", mo, tmo, [128, KC, TB])
                P.op("act", f_act(sq2[tb % 2], mo, AF.Square), reads=[tmo], writes=[t_sq2[tb % 2]])
                pend = tb
            g2_rest(pend)
            if l == 0:
                dbg_dump("hmid0", hT[:], t_h[3], [128, KC, S])

            P.barrier(["pe", "act", "dve", "pool"] + (["sp"] if dbg else []))
            AR.off = 0
            o_xs = [AR.alloc(SLOT_EL * 2) for _ in range(2)]
            ring_f = Ring([s_[:] for s_ in slots] + [AR.bf(o, SLOT_EL) for o in o_xs])
            o_hn2 = AR.alloc(KC * TB * 2)
            hn2 = AR.bf(o_hn2, KC * TB).rearrange("p (c t) -> p c t", t=TB)
            t_hn2 = T("hn2")
            o_g = AR.alloc(32 * TB * 2)
            gT = AR.bf(o_g, 32 * TB).rearrange("p (c t) -> p c t", t=TB)
            t_g = [T("g%d" % i) for i in range(4)]
            o_ff = AR.alloc(KC * TB * 4)
            ffT = AR.f32(o_ff, KC * TB).rearrange("p (c t) -> p c t", t=TB)
            t_ff = T("ffT")
            NU = 3
            o_U = [[AR.alloc((TB + 2) * 4) for _ in range(2)] for _ in range(NU)]
            U = [[AR.f32(o, TB + 2) for o in os_] for os_ in o_U]
            t_U = [[T("U%d_%d" % (a_, b_)) for b_ in range(2)] for a_ in range(NU)]
            o_Y = [[AR.alloc(TB * 4) for _ in range(2)] for _ in range(NU)]
            Y = [[AR.f32(o, TB) for o in os_] for os_ in o_Y]
            t_Y = [[T("Y%d_%d" % (a_, b_)) for b_ in range(2)] for a_ in range(NU)]
            o_cr = AR.alloc(64 * 2 * 4)
            carry = AR.f32(o_cr, 128).rearrange("p (c t) -> p c t", t=2)
            t_cr = T("carry")
            t_crs = [T("carry%d" % i) for i in range(64)]
            o_sq = AR.alloc(KC * TB * 2)
            sq = AR.bf(o_sq, KC * TB).rearrange("p (c t) -> p c t", t=TB)
            t_sq = T("sqf")
            o_rs = [AR.alloc(TB * 4) for _ in range(2)]
            rs_pre, rs_post = AR.f32(o_rs[0], TB), AR.f32(o_rs[1], TB)
            t_rpre, t_rpost = T("rpre"), T("rpost")
            P.op("dve", f_memset(carry, 0.0), writes=t_crs)
            cw = lambda j, c: vec[:, vb + V_CW + j * 64 + c:vb + V_CW + j * 64 + c + 1]
            cb = lambda c: vec[:, vb + V_CB + c:vb + V_CB + c + 1]
            A_BANKS, B_BANKS = [0, 1, 4], [2, 3, 5]

            def pre_norm_sq(tb):
                tsl = slice(tb * TB, (tb + 1) * TB)
                P.op("act", f_act(sq, hT[:, :, tsl], AF.Square), reads=[t_h[tb]], writes=[t_sq])

            def pre_norm_rest(tb, bank):
                tsl = slice(tb * TB, (tb + 1) * TB)
                items = [(banks[bank][:, :], ones_bf, sq[:, c, :], c == 0, c == KC - 1, {}) for c in range(KC)]
                P.op("pe", f_mms(items), reads=[t_sq, t_cst], writes=[t_bank[bank]])
                P.op("act", f_act(rs_pre, banks[bank][:, :], AF.Ln, bias=eps_ap, scale=1.0 / D),
                     reads=[t_bank[bank], t_small], writes=[t_rpre])
                P.op("act", f_act(rs_pre, rs_pre, AF.Exp, scale=-0.5), reads=[t_rpre], writes=[t_rpre])
                for c in range(KC):
                    P.op("dve", f_stt(hn2[:, c, :], hT[:, c, tsl], vec[:, vb + V_NPF + c:vb + V_NPF + c + 1],
                                      rs_pre, ALU.mult, ALU.mult), reads=[t_h[tb], t_rpre, t_vec], writes=[t_hn2])

            def post_norm_sq(tb):
                P.op("act", f_act(sq, ffT, AF.Square), reads=[t_ff], writes=[t_sq])

            def post_norm_rest(tb, bank):
                tsl = slice(tb * TB, (tb + 1) * TB)
                items = [(banks[bank][:, :], ones_bf, sq[:, c, :], c == 0, c == KC - 1, {}) for c in range(KC)]
                P.op("pe", f_mms(items), reads=[t_sq, t_cst], writes=[t_bank[bank]])
                P.op("act", f_act(rs_post, banks[bank][:, :], AF.Ln, bias=eps_ap, scale=1.0 / D),
                     reads=[t_bank[bank], t_small], writes=[t_rpost])
                P.op("act", f_act(rs_post, rs_post, AF.Exp, scale=-0.5), reads=[t_rpost], writes=[t_rpost])
                P.op("dve", f_tt(ffT, ffT, rs_post.unsqueeze(1).to_broadcast([128, KC, TB]), ALU.mult),
                     reads=[t_ff, t_rpost], writes=[t_ff])
                for c in range(KC):
                    P.op("dve", f_stt(hT[:, c, tsl], ffT[:, c, :], vec[:, vb + V_NPFF + c:vb + V_NPFF + c + 1], hT[:, c, tsl],
                                      ALU.mult, ALU.add), reads=[t_ff, t_vec, t_h[tb]], writes=[t_h[tb]])

            pi = 0
            prev_pair = None
            pre_norm_sq(0)
            pre_norm_rest(0, 6)
            pending_post = None
            for tb in range(NB):
                for s_ in range(8):
                    sla, t_sla = ring_f.load(lambda ap, s_=s_: [(ap[:, 0:KC * 512].rearrange("p (k c) -> p k c", c=512),
                                                                 w_up[l].rearrange("(k p) c -> p k c", p=128)[:, :, s_ * 512:(s_ + 1) * 512])])
                    slb, t_slb = ring_f.load(lambda ap, s_=s_: [(ap[:, 0:KC * 512].rearrange("p (k c) -> p k c", c=512),
                                                                 w_up[l].rearrange("(k p) c -> p k c", p=128)[:, :, DFF + s_ * 512:DFF + (s_ + 1) * 512])])
                    a3 = sla[:, 0:KC * 512].rearrange("p (k c) -> p k c", c=512)
                    b3 = slb[:, 0:KC * 512].rearrange("p (k c) -> p k c", c=512)
                    for i in range(4):
                        ca = s_ * 4 + i
                        u_ = pi % NU
                        pi += 1
                        cur = []
                        for ab, (w3, t_w, cc) in enumerate([(a3, t_sla, ca), (b3, t_slb, 32 + ca)]):
                            bk_ = (A_BANKS if ab == 0 else B_BANKS)[u_]
                            Ub, tU = U[u_][ab], t_U[u_][ab]
                            P.op("act", f_acopy(Ub[:, 0:2], carry[:, cc, :]), reads=[t_crs[cc]], writes=[tU])
                            items = [(banks[bk_][:, :], w3[:, kc, i * 128:(i + 1) * 128], hn2[:, kc, :], kc == 0, kc == KC - 1, {})
                                     for kc in range(KC)]
                            P.op("pe", f_mms(items), reads=[t_w, t_hn2], writes=[t_bank[bk_]])
                            cur.append((bk_, cc, Ub, Y[u_][ab], tU, t_Y[u_][ab]))
                        for (bk_, cc, Ub, Yb, tU, tY) in cur:
                            P.op("act", f_acopy(Ub[:, 2:TB + 2], banks[bk_][:, :]), reads=[t_bank[bk_]], writes=[tU])
                            P.op("act", f_act(Yb, banks[bk_][:, :], AF.Identity, bias=cb(cc), scale=cw(2, cc)),
                                 reads=[t_bank[bk_], t_vec], writes=[tY])
                            P.op("act", f_acopy(carry[:, cc, :], banks[bk_][:, TB - 2:TB]), reads=[t_bank[bk_]], writes=[t_crs[cc]])
                        if prev_pair is not None:
                            pu, pca = prev_pair
                            P.op("act", f_act(Y[pu][0], Y[pu][0], AF.Gelu_apprx_tanh), reads=[t_Y[pu][0]], writes=[t_Y[pu][0]])
                        for (bk_, cc, Ub, Yb, tU, tY) in cur:
                            P.op("dve", f_stt(Yb, Ub[:, 1:TB + 1], cw(1, cc), Yb, ALU.mult, ALU.add), reads=[tU, tY, t_vec], writes=[tY])
                            P.op("dve", f_stt(Yb, Ub[:, 0:TB], cw(0, cc), Yb, ALU.mult, ALU.add), reads=[tU, tY, t_vec], writes=[tY])
                        if prev_pair is not None:
                            pu, pca = prev_pair
                            P.op("dve", f_tt(gT[:, pca, :], Y[pu][0], Y[pu][1], ALU.mult), reads=[t_Y[pu][0], t_Y[pu][1]], writes=[t_g[pca // 8]])
                        prev_pair = (u_, ca)
                    if s_ == 0 and pending_post is not None:
                        post_norm_rest(pending_post, 7)
                        pending_post = None
                pu, pca = prev_pair
                P.op("act", f_act(Y[pu][0], Y[pu][0], AF.Gelu_apprx_tanh), reads=[t_Y[pu][0]], writes=[t_Y[pu][0]])
                P.op("dve", f_tt(gT[:, pca, :], Y[pu][0], Y[pu][1], ALU.mult), reads=[t_Y[pu][0], t_Y[pu][1]], writes=[t_g[pca // 8]])
                prev_pair = None
                if l == 0 and tb == 0:
                    dbg_dump("g0", gT, t_g[3], [128, 32, TB])
                if tb + 1 < NB:
                    pre_norm_sq(tb + 1)
                for jh in range(2):
                    for kq in range(4):
                        sld, t_sld = ring_f.load(lambda ap, jh=jh, kq=kq: [(ap[:, 0:KC * 512].rearrange("p (k c) -> p k c", c=512),
                                                                           w_dn[l].rearrange("(k p) c -> p k c", p=128)[:, kq * 8:(kq + 1) * 8, jh * 512:(jh + 1) * 512])])
                        d3 = sld[:, 0:KC * 512].rearrange("p (k c) -> p k c", c=512)
                        items = []
                        for jj in range(4):
                            for kc in range(KC):
                                items.append((banks[4 + jj][:, :], d3[:, kc, jj * 128:(jj + 1) * 128], gT[:, kq * 8 + kc, :],
                                              kq == 0 and kc == 0, kq == 3 and kc == KC - 1, {}))
                        P.op("pe", f_mms(items), reads=[t_sld, t_g[kq]], writes=[t_bank[4 + jj_] for jj_ in range(4)])
                    for jj in range(4):
                        P.op("act", f_acopy(ffT[:, jh * 4 + jj, :], banks[4 + jj][:, :]), reads=[t_bank[4 + jj]], writes=[t_ff])
                    if jh == 0 and tb + 1 < NB:
                        pre_norm_rest(tb + 1, 0)
                if l == 0 and tb == 0:
                    dbg_dump("ff0", ffT, t_ff, [128, KC, TB])
                post_norm_sq(tb)
                if tb + 1 < NB:
                    pending_post = tb
                else:
                    post_norm_rest(tb, 7)
            P.barrier(["pe", "act", "dve", "pool"] + (["sp"] if dbg else []))

        s_out = P.new_dma_sem()
        t_out = T("out")
        for c in range(KC):
            P.dma("sp", f_dma(outT[c * 128:(c + 1) * 128, :], hT[:, c, :]), s_out, reads=t_h, writes=[t_out])
        P.barrier(["sp", "pool", "act", "dve", "pe"])
        P.emit(nc, st)
    return nc, dbg_out


_CACHE = {}


def _get_prog(n_layers=2, dbg=()):
    key = (n_layers, tuple(dbg))
    if key not in _CACHE:
        _CACHE[key] = build(n_layers, dbg)
    return _CACHE[key]


def make_in_maps(inp):
    consts = make_consts()
    vecs = make_vecs(inp)
    shared = {
        "w_in": np.ascontiguousarray(inp["w_in"], dtype=np.float32),
        "w_a": np.ascontiguousarray(inp["w_branch_a"], dtype=np.float32),
        "w_b": np.ascontiguousarray(inp["w_branch_b"], dtype=np.float32),
        "w_c": np.ascontiguousarray(inp["w_branch_c"], dtype=np.float32),
        "w_o": np.ascontiguousarray(inp["w_out"], dtype=np.float32),
        "w_up": np.ascontiguousarray(inp["w_ffn_up"], dtype=np.float32),
        "w_dn": np.ascontiguousarray(inp["w_ffn_down"], dtype=np.float32),
        "vecs": vecs,
        "consts": consts,
    }
    maps = []
    for b in range(8):
        m = dict(shared)
        m["xT"] = np.ascontiguousarray(np.asarray(inp["x"][b], dtype=np.float32).T)
        maps.append(m)
    return maps


def kernel(**inputs):
    inp = {k: np.asarray(v) for k, v in inputs.items()}
    nc, _ = _get_prog(2, ())
    in_maps = make_in_maps(inp)
    res = run_bass_kernel_spmd(nc, in_maps, core_ids=list(range(8)))
    out = np.stack([np.ascontiguousarray(res.results[b]["outT"].T) for b in range(8)], axis=0)
    return out.astype(np.float32)
```
